# Optimizing a Trainium2 kernel written in Bass

```python
import jax, jax.numpy as jnp
from jax import lax
import numpy as np

D_MODEL = 1024
BATCH = 16
SEQ = 256
DEPTH = 4
DEC_BATCH = 4
DEC_SEQ = 1024
PAST_LEN = 256

GRID_W = 64
HEAD_DIM = 64
CHUNK = 128
A_WIDTH = D_MODEL // 2
A_GROUPS = 4
A_GROUP_CH = A_WIDTH // A_GROUPS
B_HEADS = (D_MODEL // 2) // HEAD_DIM
B_KV_HEADS = B_HEADS // 4
C_HEADS = D_MODEL // HEAD_DIM
NA_MAX_ROWS = 8
NA_COLS = 16
D_FF = ((8 * D_MODEL // 3 + 255) // 256) * 256
ROPE_BASE = 10000.0
EPS = 1e-6
Q_BLOCK = 128
N_EVEN = (DEPTH + 1) // 2
N_ODD = DEPTH // 2
EVEN_SPLITS = (A_WIDTH, 2 * A_WIDTH, 2 * A_WIDTH + B_HEADS * HEAD_DIM, 2 * A_WIDTH + (B_HEADS + B_KV_HEADS) * HEAD_DIM)
EVEN_IN = 2 * A_WIDTH + (B_HEADS + 2 * B_KV_HEADS) * HEAD_DIM
ODD_IN = 3 * C_HEADS * HEAD_DIM
NEG = -1e30

kernel_name = 'hybrid_gmlp_gqa_natten_prefix_dit_step'


def rms_norm(x, g):
    xf = x.astype(jnp.float32)
    y = xf * lax.rsqrt(jnp.mean(xf * xf, axis=-1, keepdims=True) + EPS)
    return (y * g.astype(jnp.float32)).astype(x.dtype)


def layer_norm(x, g):
    xf = x.astype(jnp.float32)
    mu = jnp.mean(xf, axis=-1, keepdims=True)
    var = jnp.mean(jnp.square(xf - mu), axis=-1, keepdims=True)
    return ((xf - mu) * lax.rsqrt(var + EPS) * g.astype(jnp.float32)).astype(x.dtype)


def modulation(cond, w_mod, b_mod):
    m = jax.nn.silu(cond) @ w_mod + b_mod
    return jnp.split(m[:, None, :], 6, axis=-1)


def sublayer_in(x, g, shift, scale):
    return rms_norm(x, g) * (1 + scale) + shift


def sublayer_out(x, y, g, gate):
    return x + gate * rms_norm(y, g)


def axial_rope(x):
    s = x.shape[1]
    t = jnp.arange(s)
    half = HEAD_DIM // 2
    nf = half // 2
    freqs = ROPE_BASE ** (-jnp.arange(nf, dtype=jnp.float32) / nf)

    def rot(xa, pos):
        ang = pos.astype(jnp.float32)[:, None] * freqs[None, :]
        cos = jnp.cos(ang)[None, :, None, :]
        sin = jnp.sin(ang)[None, :, None, :]
        x1 = xa[..., :nf].astype(jnp.float32)
        x2 = xa[..., nf:].astype(jnp.float32)
        return jnp.concatenate([x1 * cos - x2 * sin, x2 * cos + x1 * sin], axis=-1)

    out = jnp.concatenate([rot(x[..., :half], t // GRID_W), rot(x[..., half:], t % GRID_W)], axis=-1)
    return out.astype(x.dtype)


def blocked_attention(q, k, v):
    b, sq, h, d = q.shape
    hk = k.shape[2]
    g = h // hk
    nb = sq // Q_BLOCK
    scale = d ** -0.5
    qb = q.reshape(b, nb, Q_BLOCK, hk, g, d).transpose(1, 0, 2, 3, 4, 5)

    def one_block(qblk):
        s = jnp.einsum('bqngd,bsnd->bngqs', qblk, k).astype(jnp.float32) * scale
        p = jax.nn.softmax(s, axis=-1).astype(v.dtype)
        return jnp.einsum('bngqs,bsnd->bqngd', p, v)

    o = lax.map(one_block, qb)
    return o.transpose(1, 0, 2, 3, 4, 5).reshape(b, sq, h, d)


def chunk_gmlp(u, v, sgu_w, sgu_b, sgu_norm):
    b, s, _ = u.shape
    n = s // CHUNK
    u = jax.nn.gelu(u)
    v = layer_norm(jax.nn.gelu(v), sgu_norm)
    vc = v.reshape(b, n, CHUNK, A_GROUPS, A_GROUP_CH)
    mixed = jnp.einsum('gpq,bnqgc->bnpgc', sgu_w, vc) + sgu_b.T[None, None, :, :, None]
    return u * mixed.reshape(b, s, A_WIDTH)


def even_mixer(h, w_in, w_out, sgu_w, sgu_b, sgu_norm, q_norm, k_norm, ctx_k=None, ctx_v=None):
    b, s, _ = h.shape
    z = h @ w_in
    u, v, q, k, vals = jnp.split(z, EVEN_SPLITS, axis=-1)
    a_out = chunk_gmlp(u, v, sgu_w, sgu_b, sgu_norm)
    q = rms_norm(q.reshape(b, s, B_HEADS, HEAD_DIM), q_norm)
    k = rms_norm(k.reshape(b, s, B_KV_HEADS, HEAD_DIM), k_norm)
    vals = vals.reshape(b, s, B_KV_HEADS, HEAD_DIM)
    if ctx_k is None:
        b_out = blocked_attention(q, k, vals)
    else:
        keys = jnp.concatenate([axial_rope(k), ctx_k.astype(k.dtype)], axis=1)
        values = jnp.concatenate([vals, ctx_v.astype(vals.dtype)], axis=1)
        b_out = blocked_attention(axial_rope(q), keys, values)
    y = jnp.concatenate([a_out, b_out.reshape(b, s, B_HEADS * HEAD_DIM)], axis=-1) @ w_out
    return y, k, vals


def neighbourhood_attention(q, k, v, ctx_k, ctx_v, rpb):
    b, s, h, d = q.shape
    rows = s // GRID_W
    wr = min(NA_MAX_ROWS, rows)
    scale = d ** -0.5
    qg = q.reshape(b, rows, GRID_W, h, d)
    kg = k.reshape(b, rows, GRID_W, h, d)
    vg = v.reshape(b, rows, GRID_W, h, d)
    cols = jnp.arange(GRID_W)
    col_start = jnp.clip(cols - NA_COLS // 2, 0, GRID_W - NA_COLS)
    col_in = (cols[None, :] >= col_start[:, None]) & (cols[None, :] < col_start[:, None] + NA_COLS)
    col_idx = jnp.clip(cols[None, :] - cols[:, None] + NA_COLS - 1, 0, 2 * NA_COLS - 2)
    rpb = rpb.astype(jnp.float32)

    def one_row(r):
        rs = jnp.clip(r - wr // 2, 0, rows - wr)
        qr = lax.dynamic_index_in_dim(qg, r, axis=1, keepdims=False)
        kb = lax.dynamic_slice_in_dim(kg, rs, wr, axis=1)
        vb = lax.dynamic_slice_in_dim(vg, rs, wr, axis=1)
        s_win = jnp.einsum('bqhd,brwhd->bhqrw', qr, kb).astype(jnp.float32) * scale
        row_idx = rs + jnp.arange(wr) - r + NA_MAX_ROWS - 1
        bias = rpb[:, row_idx][:, :, col_idx].transpose(0, 2, 1, 3)
        s_win = jnp.where(col_in[None, None, :, None, :], s_win + bias[None], NEG)
        s_ctx = jnp.einsum('bqhd,bchd->bhqc', qr, ctx_k).astype(jnp.float32) * scale
        scores = jnp.concatenate([s_win.reshape(b, h, GRID_W, wr * GRID_W), s_ctx], axis=-1)
        p = jax.nn.softmax(scores, axis=-1).astype(v.dtype)
        p_win = p[..., :wr * GRID_W].reshape(b, h, GRID_W, wr, GRID_W)
        p_ctx = p[..., wr * GRID_W:]
        return jnp.einsum('bhqrw,brwhd->bqhd', p_win, vb) + jnp.einsum('bhqc,bchd->bqhd', p_ctx, ctx_v)

    o = lax.map(one_row, jnp.arange(rows))
    return o.transpose(1, 0, 2, 3, 4).reshape(b, s, h, d)


def odd_mixer(h, w_in, w_out, rpb, ctx_k=None, ctx_v=None):
    b, s, _ = h.shape
    q, k, v = jnp.split(h @ w_in, 3, axis=-1)
    q = q.reshape(b, s, C_HEADS, HEAD_DIM)
    k = k.reshape(b, s, C_HEADS, HEAD_DIM)
    v = v.reshape(b, s, C_HEADS, HEAD_DIM)
    if ctx_k is None:
        o = blocked_attention(q, k, v)
    else:
        o = neighbourhood_attention(q, k, v, ctx_k.astype(k.dtype), ctx_v.astype(v.dtype), rpb)
    return o.reshape(b, s, C_HEADS * HEAD_DIM) @ w_out, k, v


def swiglu(h, w_in, w_out):
    g, u = jnp.split(h @ w_in, 2, axis=-1)
    return (jax.nn.silu(g) * u) @ w_out


def setup_inputs(seed: int = 0) -> dict:
    key = jax.random.key(seed)
    ks = iter(jax.random.split(key, 32))

    def nrm(shape, scale):
        return jax.random.normal(next(ks), shape, jnp.float32) * scale

    def gain(shape):
        return 1.0 + nrm(shape, 0.01)

    d = D_MODEL
    return {
        'x_prompt': nrm((BATCH, SEQ, d), 1.0),
        'x_sample': nrm((DEC_BATCH, DEC_SEQ, d), 1.0),
        'cache_attn_k': nrm((DEC_BATCH, N_EVEN, PAST_LEN, B_KV_HEADS, HEAD_DIM), 1.0),
        'cache_attn_v': nrm((DEC_BATCH, N_EVEN, PAST_LEN, B_KV_HEADS, HEAD_DIM), 1.0),
        'cache_na_k': nrm((DEC_BATCH, N_ODD, PAST_LEN, C_HEADS, HEAD_DIM), 1.0),
        'cache_na_v': nrm((DEC_BATCH, N_ODD, PAST_LEN, C_HEADS, HEAD_DIM), 1.0),
        'c': nrm((DEC_BATCH, d), 1.0),
        'c_ctx': nrm((d,), 1.0),
        'mod_w': nrm((DEPTH, d, 6 * d), 0.5 * d ** -0.5),
        'mod_b': nrm((DEPTH, 6 * d), 0.01),
        'norm_mix_pre': gain((DEPTH, d)),
        'norm_mix_post': gain((DEPTH, d)),
        'norm_ffn_pre': gain((DEPTH, d)),
        'norm_ffn_post': gain((DEPTH, d)),
        'even_w_in': nrm((N_EVEN, d, EVEN_IN), d ** -0.5),
        'even_w_out': nrm((N_EVEN, d, d), d ** -0.5),
        'sgu_w': nrm((N_EVEN, A_GROUPS, CHUNK, CHUNK), CHUNK ** -0.5),
        'sgu_b': gain((N_EVEN, A_GROUPS, CHUNK)),
        'sgu_norm': gain((N_EVEN, A_WIDTH)),
        'q_norm': gain((N_EVEN, HEAD_DIM)),
        'k_norm': gain((N_EVEN, HEAD_DIM)),
        'odd_w_in': nrm((N_ODD, d, ODD_IN), d ** -0.5),
        'odd_w_out': nrm((N_ODD, d, d), d ** -0.5),
        'na_rpb': nrm((N_ODD, C_HEADS, 2 * NA_MAX_ROWS - 1, 2 * NA_COLS - 1), 0.02),
        'ffn_w_in': nrm((DEPTH, d, 2 * D_FF), d ** -0.5),
        'ffn_w_out': nrm((DEPTH, D_FF, d), D_FF ** -0.5),
    }


def reference(x_prompt, x_sample, cache_attn_k, cache_attn_v, cache_na_k, cache_na_v, c, c_ctx,
              mod_w, mod_b, norm_mix_pre, norm_mix_post, norm_ffn_pre, norm_ffn_post,
              even_w_in, even_w_out, sgu_w, sgu_b, sgu_norm, q_norm, k_norm,
              odd_w_in, odd_w_out, na_rpb, ffn_w_in, ffn_w_out):
    xp, xs = x_prompt, x_sample
    attn_k, attn_v, na_k, na_v = [], [], [], []
    for l in range(DEPTH):
        sh_p, sc_p, gt_p, fsh_p, fsc_p, fgt_p = modulation(c_ctx[None, :], mod_w[l], mod_b[l])
        sh_s, sc_s, gt_s, fsh_s, fsc_s, fgt_s = modulation(c, mod_w[l], mod_b[l])
        hp = sublayer_in(xp, norm_mix_pre[l], sh_p, sc_p)
        hs = sublayer_in(xs, norm_mix_pre[l], sh_s, sc_s)
        if l % 2 == 0:
            e = l // 2
            yp, kp, vp = even_mixer(hp, even_w_in[e], even_w_out[e], sgu_w[e], sgu_b[e], sgu_norm[e], q_norm[e], k_norm[e])
            ys, _, _ = even_mixer(hs, even_w_in[e], even_w_out[e], sgu_w[e], sgu_b[e], sgu_norm[e], q_norm[e], k_norm[e],
                                  ctx_k=cache_attn_k[:, e], ctx_v=cache_attn_v[:, e])
            attn_k.append(kp)
            attn_v.append(vp)
        else:
            o = l // 2
            yp, kp, vp = odd_mixer(hp, odd_w_in[o], odd_w_out[o], na_rpb[o])
            ys, _, _ = odd_mixer(hs, odd_w_in[o], odd_w_out[o], na_rpb[o], ctx_k=cache_na_k[:, o], ctx_v=cache_na_v[:, o])
            na_k.append(kp)
            na_v.append(vp)
        xp = sublayer_out(xp, yp, norm_mix_post[l], gt_p)
        xs = sublayer_out(xs, ys, norm_mix_post[l], gt_s)
        fp = swiglu(sublayer_in(xp, norm_ffn_pre[l], fsh_p, fsc_p), ffn_w_in[l], ffn_w_out[l])
        fs = swiglu(sublayer_in(xs, norm_ffn_pre[l], fsh_s, fsc_s), ffn_w_in[l], ffn_w_out[l])
        xp = sublayer_out(xp, fp, norm_ffn_post[l], fgt_p)
        xs = sublayer_out(xs, fs, norm_ffn_post[l], fgt_s)
    new_attn_k = jnp.stack(attn_k, axis=1)
    new_attn_v = jnp.stack(attn_v, axis=1)
    new_na_k = jnp.stack(na_k, axis=1)
    new_na_v = jnp.stack(na_v, axis=1)
    return (xp, xs, new_attn_k, new_attn_v, new_na_k, new_na_v)
```

```python
import contextlib
import math
import os
import numpy as np
import concourse.bass as bass
import concourse.mybir as mybir
from concourse.bass_utils import run_bass_kernel_spmd

F32 = mybir.dt.float32
BF16 = mybir.dt.bfloat16
AF = mybir.ActivationFunctionType
ALU = mybir.AluOpType

D = 1024
T = 1024
DFF = 2816
NJ = DFF // 128
EPS = 1e-6
BIG = 32768.0
CMASK = -30000.0
SB_BASE = 16896
SB_TOP = 229344
BS = 512
ENGS = ("pe", "act", "dve", "pool", "sp")
NAUG = 17
SMALL = 1 << 40


class View:
    __slots__ = ("ap", "blk", "n")

    def __init__(self, ap, blk):
        self.ap = ap
        self.blk = blk
        n = 1
        for d in tuple(ap.shape)[1:]:
            n *= int(d)
        self.n = n


class Buf:
    def __init__(self, nc, name, free_shape, dtype, off, parts=128):
        self.esz = 2 if dtype == BF16 else 4
        self.h = nc.alloc_sbuf_tensor_at(name, [parts] + list(free_shape), dtype, offset=off)
        self.off = off
        self.shape = list(free_shape)
        st = []
        s = 1
        for d in reversed(self.shape):
            st.insert(0, s)
            s *= d
        self.strides = st
        self.fsize = s
        self.nbytes = s * self.esz

    def rng(self, idx):
        idx = list(idx) + [slice(None)] * (len(self.shape) - len(idx))
        lo = hi = 0
        for i, d, st in zip(idx, self.shape, self.strides):
            if isinstance(i, slice):
                a = 0 if i.start is None else i.start
                b = d if i.stop is None else i.stop
                lo += a * st
                hi += (b - 1) * st
            else:
                lo += i * st
                hi += i * st
        return idx, lo, hi

    def blocks(self, lo, hi):
        b0 = (self.off + lo * self.esz) // BS
        b1 = (self.off + (hi + 1) * self.esz - 1) // BS
        return frozenset(("s", b) for b in range(b0, b1 + 1))

    def v(self, *idx, p=slice(None)):
        idx, lo, hi = self.rng(idx)
        return View(self.h[tuple([p] + idx)], self.blocks(lo, hi))

    def raw(self, elem_off, dims, lo, hi, p0=0):
        ap = bass.AP(self.h, p0 * self.fsize + elem_off, [[self.fsize, dims[0][1]]] + [list(x) for x in dims[1:]])
        return View(ap, self.blocks(lo, hi))


class Op:
    __slots__ = ("eng", "fn", "deps", "dma_key", "ticket", "signal", "idx", "size")


class Prog:
    def __init__(self):
        self.ops = []
        self.lastw = {}
        self.readers = {}
        self.group_keys = set()
        self.ambient = ()

    def op(self, eng, fn, reads=(), writes=(), dma_key=None, extra_deps=()):
        o = Op()
        o.eng = eng
        o.fn = fn
        o.dma_key = dma_key
        o.signal = False
        o.ticket = None
        o.idx = len(self.ops)
        o.size = min([v.n for v in writes] + [1 << 30])
        rb = set()
        wb = set()
        for v in reads:
            rb |= v.blk
        for v in writes:
            wb |= v.blk
        deps = set(extra_deps) | set(self.ambient)
        for b in rb:
            w = self.lastw.get(b)
            if w is not None:
                deps.add(w)
        for b in wb:
            w = self.lastw.get(b)
            if w is not None:
                deps.add(w)
            for r in self.readers.get(b, ()):
                deps.add(r)
        for b in wb:
            self.lastw[b] = o
            self.readers[b] = []
        for b in rb - wb:
            self.readers.setdefault(b, []).append(o)
        deps.discard(o)
        o.deps = deps
        self.ops.append(o)
        return o

    @staticmethod
    def _skip(d, o):
        if d.dma_key is not None or o.dma_key is not None or d.eng != o.eng:
            return False
        if d.eng == "pe":
            return True
        return d.size > SMALL and o.size > SMALL

    def finalize(self, nc, es):
        for o in self.ops:
            for d in o.deps:
                if not self._skip(d, o):
                    d.signal = True
        engcnt = {e: 0 for e in ENGS}
        keycnt = {}
        for o in self.ops:
            if o.dma_key is not None:
                keycnt[o.dma_key] = keycnt.get(o.dma_key, 0) + 16
                o.ticket = (("k", o.dma_key), keycnt[o.dma_key])
            elif o.signal and o.fn is not None:
                engcnt[o.eng] += 1
                o.ticket = (("e", o.eng), engcnt[o.eng])
        for o in self.ops:
            if o.dma_key is not None and o.dma_key in self.group_keys:
                o.ticket = (("k", o.dma_key), keycnt[o.dma_key])
        for o in self.ops:
            for d in o.deps:
                if d.dma_key is not None and d.dma_key in self.group_keys:
                    assert o.dma_key != d.dma_key, "group-key self dependency %s" % d.dma_key
        sems = {}
        for o in self.ops:
            if o.ticket is not None and o.ticket[0] not in sems:
                sems[o.ticket[0]] = es.enter_context(nc.semaphore("s_%s_%s" % o.ticket[0]))
        self.stats = (dict(engcnt), dict(keycnt), len(sems))
        block = es.enter_context(nc.Block())
        engmap = {"pe": block.tensor, "act": block.scalar, "dve": block.vector, "pool": block.gpsimd, "sp": block.sync}
        for en in ENGS:
            mine = [o for o in self.ops if o.eng == en]

            def f(eng, mine=mine):
                seen = {}
                for o in mine:
                    need = {}
                    for d in o.deps:
                        if self._skip(d, o) or d.ticket is None:
                            continue
                        sk, val = d.ticket
                        if need.get(sk, 0) < val:
                            need[sk] = val
                    for sk, val in need.items():
                        if seen.get(sk, 0) < val:
                            eng.wait_ge(sems[sk], val)
                            seen[sk] = val
                    if o.fn is None:
                        continue
                    ins = o.fn(eng)
                    if o.ticket is not None:
                        ins.then_inc(sems[o.ticket[0]], 16 if o.dma_key is not None else 1)

            engmap[en](f)


def na_tile_ranges():
    def rs(r):
        return min(max(r - 4, 0), 8)
    lo_hi = []
    for j in range(8):
        lo = rs(2 * j) // 2
        hi = (rs(2 * j + 1) + 7) // 2
        lo_hi.append((lo, hi))
    out = []
    for i in range(8):
        js = [j for j in range(8) if lo_hi[j][0] <= i <= lo_hi[j][1]]
        assert js == list(range(js[0], js[-1] + 1))
        for j in js:
            assert -3 <= i - j <= 3
        out.append((js[0], js[-1]))
    return out


class _Stop(Exception):
    pass


def build_nc(nlayers=4, stop=None):
    oddstop = int(os.environ.get("MK_ODDSTOP", "0"))
    nc = bass.Bass("TRN2", target_bir_lowering=False)
    P = Prog()
    es = contextlib.ExitStack()

    def din(name, shape):
        return nc.dram_tensor(name, list(shape), F32, kind="ExternalInput").ap()

    def dout(name, shape):
        return nc.dram_tensor(name, list(shape), F32, kind="ExternalOutput").ap()

    d_xT = din("xT", [128, 8, T])
    d_cond = din("condT", [128, 8])
    d_modw = din("mod_w", [4, D, 6 * D])
    d_modb = din("modbT", [128, 4, 48])
    d_gains = din("gainsT", [128, 4, 4, 8])
    d_ewin = din("even_w_in", [2, D, 1792])
    d_ewout = din("even_w_out", [2, D, D])
    d_owin = din("odd_w_in", [2, D, 3072])
    d_owout = din("odd_w_out", [2, D, D])
    d_fwin = din("ffn_w_in", [4, D, 2 * DFF])
    d_fwout = din("ffn_w_out", [4, DFF, D])
    d_sguw = din("sguwT", [2, 4, 128, 128])
    d_sgub = din("sgub_bc", [128, 2 * 4 * 128])
    d_sgun = din("sgun_bc", [128, 2 * 512])
    d_qkg = din("qkgain", [128, 4])
    d_perm = din("perm", [128, 128])
    d_cos = din("cosT", [128, T])
    d_sin = din("sinT", [128, T])
    d_qaug = din("qaug", [2 * NAUG, T])
    d_kaug = din("kaug", [2 * NAUG, T + 256])
    d_cmask = din("colmaskT", [128, 64])
    d_rpbz = nc.dram_tensor("rpbz", [2 * 16 * 15 * 2048], F32, kind="ExternalInput")
    d_ckTe = din("ckT_e", [2, 128, 256])
    d_cve = din("cv_e", [2, 256, 128])
    d_ckTo = din("ckT_o", [2, 16, 64, 256])
    d_cvo = din("cv_o", [2, 256, 1024])
    o_yT = dout("yT", [128, 8, T])
    o_kTe = dout("kT_even", [2, 128, T])
    o_ve = dout("v_even", [2, T, 128])
    o_kTo = dout("kT_odd", [2, D, T])
    o_vo = dout("v_odd", [2, T, D])

    cur = [SB_BASE]

    def alloc(nbytes):
        off = cur[0]
        cur[0] = off + ((nbytes + BS - 1) // BS) * BS
        assert cur[0] <= SB_TOP, "SBUF overflow %d" % (cur[0] - SB_TOP)
        return off

    def mk(name, shape, dt, parts=128, off=None):
        esz = 2 if dt == BF16 else 4
        n = esz
        for s in shape:
            n *= s
        if off is None:
            off = alloc(n)
        return Buf(nc, name, shape, dt, off, parts)

    xT = mk("xT", [8, T], F32)
    off_h = alloc(16384 * 2)
    hT = mk("hT", [8, T], BF16, off=off_h)
    OT = mk("OT", [8, T], BF16, off=off_h + 16384)
    yT_ffn = mk("yTf", [8, T], F32, off=off_h)
    QT = [mk("QT%d" % i, [T], BF16, parts=98) for i in range(4)]
    KT = [mk("KT%d" % i, [T + 256], BF16, parts=98) for i in range(4)]
    NSLOT = 3
    slot_off = [alloc(8192) for _ in range(NSLOT)]
    Wa = [mk("Wa%d" % i, [8, 512], BF16, off=slot_off[i]) for i in range(NSLOT)]
    Wb = [mk("Wb%d" % i, [NJ, 128], BF16, off=slot_off[i]) for i in range(NSLOT)]
    R0 = alloc(45056)
    actT = mk("actT", [NJ, T], BF16, off=R0)
    yT_mix = mk("yTm", [8, T], F32, off=R0)
    gu = mk("gu", [4, T], BF16, off=R0)
    vln = mk("vln", [8, 512], BF16, off=R0 + 8192)
    VaugE = mk("VaugE", [10, 4, 128], BF16, off=R0 + 16384)
    KoutE = mk("KoutE", [T], F32, off=R0 + 26624)
    VoutE = mk("VoutE", [8, 128], F32, off=R0 + 30720)
    VaugO = mk("VaugO", [10, 4, 128], BF16, off=R0)
    BT = mk("BT", [4, 14, 64], F32, off=R0 + 10240)
    KoutO = mk("KoutO", [2, T], F32, off=R0 + 24576)
    VoutO = mk("VoutO", [8, 256], F32, off=R0 + 32768)
    cosb = mk("cosb", [T], F32)
    sinb = mk("sinb", [T], F32)
    rstd = mk("rstd", [T], F32)
    sqb = [mk("sqb%d" % i, [512], BF16) for i in range(3)]
    tmpB = [mk("tmpB%d" % i, [512], F32) for i in range(6)]
    qgb = [mk("qgb%d" % i, [512], BF16) for i in range(2)]
    PTb = [mk("PT%d" % i, [512], BF16) for i in range(3)]
    sgub = mk("sgub", [2, 4, 128], F32)
    sgun = mk("sgun", [2, 512], F32)
    sguw = mk("sguw", [2, 4, 128], BF16)
    onesm = mk("onesm", [128], BF16)
    bdiag = mk("bdiag", [128], BF16)
    permb = mk("permb", [128], BF16)
    modT = mk("modT", [2, 48], F32)
    modb = mk("modb", [4, 48], F32)
    gains = mk("gains", [4, 4, 8], F32)
    der = mk("der", [2, 4, 8], F32)
    condb = mk("condb", [8], F32)
    scT = mk("scT", [8], BF16)
    misc = mk("misc", [8], F32)
    qkg = mk("qkg", [4], F32)
    cmask = mk("cmask", [64], F32)
    mrow = mk("mrow", [512], F32)
    stat = mk("stat", [4, 16], F32)
    sbuf_used = cur[0] - SB_BASE

    psb = [es.enter_context(nc.psum_tensor("psb%d" % i, [128, 512], F32)) for i in range(8)]

    def PS(i, p=slice(None), c=slice(None)):
        return View(psb[i][p, c], frozenset([("p", i)]))

    rotA = [0]
    rotB = [0]

    def bankA():
        b = rotA[0] % 4
        rotA[0] += 1
        return b

    def bankB():
        b = 4 + rotB[0] % 4
        rotB[0] += 1
        return b

    rot = {}

    def nxt(name, lst):
        i = rot.get(name, 0)
        rot[name] = i + 1
        return lst[i % len(lst)]

    def mm_group(out, pairs, eng="pe", extra_reads=()):
        reads = [v for pr in pairs for v in pr] + list(extra_reads)

        def fn(e, out=out, pairs=pairs):
            n = len(pairs)
            ins = None
            for k, (l, r) in enumerate(pairs):
                ins = e.matmul(out.ap, lhsT=l.ap, rhs=r.ap, start=(k == 0), stop=(k == n - 1))
            return ins
        return P.op("pe", fn, reads=reads, writes=[out])

    def mm1(out, l, r, start=True, stop=True):
        def fn(e, out=out, l=l, r=r, start=start, stop=stop):
            return e.matmul(out.ap, lhsT=l.ap, rhs=r.ap, start=start, stop=stop)
        return P.op("pe", fn, reads=[l, r], writes=[out])

    def act(out, in_, func, scale=1.0, bias=None):
        reads = [in_]
        kw = {}
        if isinstance(scale, View):
            reads.append(scale)
            kw["scale"] = scale.ap
        else:
            kw["scale"] = float(scale)
        if isinstance(bias, View):
            reads.append(bias)
            kw["bias"] = bias.ap
        elif bias is not None:
            kw["bias"] = float(bias)

        def fn(e, out=out, in_=in_, func=func, kw=kw):
            return e.activation(out=out.ap, in_=in_.ap, func=func, **kw)
        return P.op("act", fn, reads=reads, writes=[out])

    def tt(out, a, b, op, eng="dve"):
        def fn(e, out=out, a=a, b=b, op=op):
            return e.tensor_tensor(out=out.ap, in0=a.ap, in1=b.ap, op=op)
        return P.op(eng, fn, reads=[a, b], writes=[out])

    def stt(out, a, s, b, op0, op1):
        reads = [a, b]
        if isinstance(s, View):
            reads.append(s)
            sv = s.ap
        else:
            sv = float(s)

        def fn(e, out=out, a=a, b=b, sv=sv, op0=op0, op1=op1):
            return e.scalar_tensor_tensor(out=out.ap, in0=a.ap, scalar=sv, in1=b.ap, op0=op0, op1=op1)
        return P.op("dve", fn, reads=reads, writes=[out])

    def ts(out, a, s1, s2, op0, op1):
        reads = [a]
        s1v = s1.ap if isinstance(s1, View) else float(s1)
        s2v = s2.ap if isinstance(s2, View) else float(s2)
        for s in (s1, s2):
            if isinstance(s, View):
                reads.append(s)

        def fn(e, out=out, a=a, s1v=s1v, s2v=s2v, op0=op0, op1=op1):
            return e.tensor_scalar(out=out.ap, in0=a.ap, scalar1=s1v, scalar2=s2v, op0=op0, op1=op1)
        return P.op("dve", fn, reads=reads, writes=[out])

    def vcopy(out, in_, eng="dve"):
        def fn(e, out=out, in_=in_):
            return e.tensor_copy(out=out.ap, in_=in_.ap)
        return P.op(eng, fn, reads=[in_], writes=[out])

    def recip(out, in_):
        def fn(e, out=out, in_=in_):
            return e.reciprocal(out=out.ap, in_=in_.ap)
        return P.op("dve", fn, reads=[in_], writes=[out])

    def memset(v, val, eng="dve"):
        def fn(e, v=v, val=val):
            return e.memset(v.ap, val)
        return P.op(eng, fn, writes=[v])

    def dma_in(q, out, src_ap, key):
        def fn(e, out=out, src_ap=src_ap):
            return e.dma_start(out=out.ap, in_=src_ap)
        return P.op(q, fn, writes=[out], dma_key=key)

    out_dmas = []

    def dma_out(q, dst_ap, in_, key):
        def fn(e, dst_ap=dst_ap, in_=in_):
            return e.dma_start(out=dst_ap, in_=in_.ap)
        o = P.op(q, fn, reads=[in_], dma_key=key)
        out_dmas.append(o)
        return o

    slot_i = [0]

    def load_w(parts):
        s = slot_i[0] % NSLOT
        slot_i[0] += 1
        for vf, src in parts:
            dma_in("pool", vf(s), src, "w%d" % s)
        return s

    def wview(dram2d, c0, c1):
        return dram2d.rearrange("(kc p) n -> p kc n", p=128)[:, :, c0:c1]

    P.group_keys.add("const")
    for fc in range(8):
        dma_in("sp", xT.v(fc), d_xT[:, fc, :], "const")
    dma_in("sp", condb.v(), d_cond, "const")
    dma_in("sp", modb.v(), d_modb, "const")
    dma_in("sp", gains.v(), d_gains, "const")
    dma_in("sp", sgub.v(), d_sgub.rearrange("p (e g q) -> p e g q", e=2, g=4), "const")
    dma_in("sp", sgun.v(), d_sgun.rearrange("p (e c) -> p e c", e=2), "const")
    dma_in("sp", qkg.v(), d_qkg, "const")
    dma_in("sp", cosb.v(), d_cos, "const")
    dma_in("sp", sinb.v(), d_sin, "const")
    dma_in("sp", cmask.v(), d_cmask, "const")
    P.group_keys.add("constc")
    dma_in("pool", permb.v(), d_perm, "constc")
    dma_in("pool", sguw.v(), d_sguw.rearrange("e g q p -> q e g p"), "constc")
    for i in range(4):
        dma_in("pool", QT[i].v(p=slice(64, 98)), d_qaug, "constc")
        dma_in("pool", KT[i].v(p=slice(64, 98)), d_kaug, "constc")
    memset(onesm.v(), 1.0 / 1024.0)
    memset(bdiag.v(), 0.0)
    memset(bdiag.v(slice(0, 64), p=slice(0, 64)), 1.0 / 64.0)
    memset(bdiag.v(slice(64, 128), p=slice(64, 128)), 1.0 / 64.0)
    memset(misc.v(slice(0, 1)), EPS)
    memset(misc.v(slice(1, 2)), 1.0)
    eps_c = misc.v(slice(0, 1))
    one_c = misc.v(slice(1, 2), p=slice(0, 1))
    act(scT.v(), condb.v(), AF.Silu)

    def modulation_gen(l):
        mi = l % 2
        mbank = bankB()
        for n in range(12):
            s = load_w([(lambda s: Wa[s].v(), wview(d_modw[l], n * 512, (n + 1) * 512))])
            b = bankA()
            mm_group(PS(b, slice(0, 1)), [(scT.v(slice(kc, kc + 1)), Wa[s].v(kc)) for kc in range(8)])
            vcopy(mrow.v(p=slice(0, 1)), PS(b, slice(0, 1)))
            for i in range(4):
                col = n * 4 + i
                mm1(PS(mbank, slice(None), slice(col, col + 1)), mrow.v(slice(i * 128, (i + 1) * 128), p=slice(0, 1)), one_c)
            yield
        tt(modT.v(mi), PS(mbank, slice(None), slice(0, 48)), modb.v(l), ALU.add)
        m = lambda a, b: modT.v(mi, slice(a, b))
        stt(der.v(mi, 0), m(8, 16), 1.0, gains.v(0, l), ALU.add, ALU.mult)
        tt(der.v(mi, 1), m(16, 24), gains.v(1, l), ALU.mult)
        stt(der.v(mi, 2), m(32, 40), 1.0, gains.v(2, l), ALU.add, ALU.mult)
        tt(der.v(mi, 3), m(40, 48), gains.v(3, l), ALU.mult)

    def modulation(l):
        for _ in modulation_gen(l):
            pass

    def col(buf, *idx):
        return buf.v(*idx)

    def rms_stats(src):
        banks = [bankB(), bankB()]
        for th in range(2):
            for fc in range(8):
                sq = nxt("sqb", sqb)
                act(sq.v(), src.v(fc, slice(th * 512, (th + 1) * 512)), AF.Square)
                mm1(PS(banks[th]), onesm.v(), sq.v(), start=(fc == 0), stop=(fc == 7))
            t = nxt("tmpB", tmpB)
            act(t.v(), PS(banks[th]), AF.Sqrt, bias=eps_c)
            recip(rstd.v(slice(th * 512, (th + 1) * 512)), t.v())

    def sub_in(mi, ai, bcol0):
        rms_stats(xT)
        for th in range(2):
            for fc in range(8):
                sl = slice(th * 512, (th + 1) * 512)
                t = nxt("tmpB", tmpB)
                stt(t.v(), xT.v(fc, sl), der.v(mi, ai, slice(fc, fc + 1)), rstd.v(sl), ALU.mult, ALU.mult)
                act(hT.v(fc, sl), t.v(), AF.Identity, bias=modT.v(mi, slice(bcol0 + fc, bcol0 + fc + 1)))

    def sub_out(yb, mi, gi):
        rms_stats(yb)
        for th in range(2):
            for fc in range(8):
                sl = slice(th * 512, (th + 1) * 512)
                t = nxt("tmpB", tmpB)
                stt(t.v(), yb.v(fc, sl), der.v(mi, gi, slice(fc, fc + 1)), rstd.v(sl), ALU.mult, ALU.mult)
                tt(xT.v(fc, sl), xT.v(fc, sl), t.v(), ALU.add)

    def proj_fm(ws, c0, th, src=None, nk=8, wb=None):
        src = hT if src is None else src
        b = bankA()
        wbuf = Wa[ws] if wb is None else wb
        mm_group(PS(b), [(wbuf.v(kc, slice(c0, c0 + 128)), src.v(kc, slice(th * 512, (th + 1) * 512))) for kc in range(nk)])
        return b

    evac_flip = [0]

    def evac(out, bank_view):
        evac_flip[0] ^= 1
        if evac_flip[0]:
            act(out, bank_view, AF.Copy)
        else:
            vcopy(out, bank_view)

    def w_out_proj(dram2d, ybuf):
        for half in range(2):
            s = load_w([(lambda s: Wa[s].v(), wview(dram2d, half * 512, (half + 1) * 512))])
            for th in range(2):
                for c in range(4):
                    fc = half * 4 + c
                    b = proj_fm(s, c * 128, th, src=OT)
                    evac(ybuf.v(fc, slice(th * 512, (th + 1) * 512)), PS(b))

    def normrope(b, th, gcol, dstA, dstB, kout=None):
        sl = slice(th * 512, (th + 1) * 512)
        sq = nxt("sqb", sqb)
        act(sq.v(), PS(b), AF.Square)
        qg = nxt("qgb", qgb)
        a_last = act(qg.v(), PS(b), AF.Identity, scale=gcol)
        P.ambient = (a_last,)
        bs = bankB()
        mm1(PS(bs), bdiag.v(), sq.v())
        bp = bankB()
        mm1(PS(bp), permb.v(), qg.v())
        sd = nxt("tmpB", tmpB)
        act(sd.v(), PS(bs), AF.Sqrt, bias=eps_c)
        rs = nxt("tmpB", tmpB)
        recip(rs.v(), sd.v())
        if kout is not None:
            stt(kout, PS(b), gcol, rs.v(), ALU.mult, ALU.mult)
        t1 = nxt("tmpB", tmpB)
        stt(t1.v(), PS(b), gcol, cosb.v(sl), ALU.mult, ALU.mult)
        t2 = nxt("tmpB", tmpB)
        tt(t2.v(), PS(bp), sinb.v(sl), ALU.mult)
        tt(t1.v(), t1.v(), t2.v(), ALU.add)
        tt(dstA, t1.v(p=slice(0, 64)), rs.v(p=slice(0, 64)), ALU.mult)
        tt(dstB, t1.v(p=slice(64, 128)), rs.v(p=slice(64, 128)), ALU.mult)
        P.ambient = ()

    def finalize_head(obank, parity, chunk, th):
        sl = slice(th * 512, (th + 1) * 512)
        r = nxt("tmpB", tmpB)
        if parity == 0:
            recip(r.v(p=slice(0, 64)), PS(obank, slice(64, 128)))
            tt(OT.v(chunk, sl, p=slice(0, 64)), PS(obank, slice(0, 64)), r.v(p=slice(0, 64)), ALU.mult)
        else:
            recip(r.v(p=slice(64, 128)), PS(obank, slice(0, 64)))
            tt(OT.v(chunk, sl, p=slice(64, 128)), PS(obank, slice(64, 128)), r.v(p=slice(64, 128)), ALU.mult)

    def run_attn_steps(steps):
        n = len(steps)
        sb = [None] * n
        pt = [None] * n
        for k in range(n + 2):
            if k < n:
                st = steps[k]
                b = bankA()
                sb[k] = b
                l, r, w = st["S"]
                mm1(PS(b, slice(None), slice(0, w)), l, r)
            if 0 <= k - 1 < n:
                st = steps[k - 1]
                w = st["S"][2]
                p = nxt("PT", PTb)
                pt[k - 1] = p
                if st["bias"] is None:
                    act(p.v(slice(0, w)), PS(sb[k - 1], slice(None), slice(0, w)), AF.Exp, scale=0.125)
                else:
                    t = nxt("tmpB", tmpB)
                    stt(t.v(slice(0, w)), PS(sb[k - 1], slice(None), slice(0, w)), 0.125, st["bias"], ALU.mult, ALU.add)
                    act(p.v(slice(0, w)), t.v(slice(0, w)), AF.Exp)
            if 0 <= k - 2 < n:
                st = steps[k - 2]
                w = st["S"][2]
                ob, oc, vl, sta, sto = st["PV"]
                mm1(PS(ob, slice(None), oc), vl, pt[k - 2].v(slice(0, w)), start=sta, stop=sto)
                if st.get("fin") is not None:
                    finalize_head(ob, *st["fin"])

    def even_mixer(l):
        e = l // 2
        W = d_ewin[e]
        memset(VaugE.v(), 1.0)
        for kv in range(2):
            dma_in("pool", KT[kv].v(slice(T, T + 256), p=slice(0, 64)), d_ckTe[e, kv * 64:(kv + 1) * 64, :], "ctxk%d" % kv)
        for i in range(2):
            for kv in range(2):
                for par in range(2):
                    dma_in("pool", VaugE.v(8 + i, kv * 2 + par, slice(par * 64, par * 64 + 64)),
                           d_cve[e, i * 128:(i + 1) * 128, kv * 64:(kv + 1) * 64], "ctxv%d" % (i * 4 + kv * 2 + par))
        s = load_w([(lambda s: Wa[s].v(), wview(W, 0, 512))])
        for th in range(2):
            for c in range(4):
                b = proj_fm(s, c * 128, th)
                act(gu.v(c, slice(th * 512, (th + 1) * 512)), PS(b), AF.Gelu_apprx_tanh)
        s = load_w([(lambda s: Wa[s].v(), wview(W, 512, 1024))])
        for tt_ in range(8):
            b = bankA()
            mm_group(PS(b), [(hT.v(kc, slice(tt_ * 128, (tt_ + 1) * 128)), Wa[s].v(kc)) for kc in range(8)])
            vg = nxt("tmpB", tmpB)
            act(vg.v(), PS(b), AF.Gelu_apprx_tanh)
            sv = stat.v(tt_ % 4)
            st6 = stat.v(tt_ % 4, slice(0, 6))
            mv = stat.v(tt_ % 4, slice(6, 8))

            def f1(eng, st6=st6, vg=vg):
                return eng.bn_stats(out=st6.ap, in_=vg.v().ap)
            P.op("dve", f1, reads=[vg.v()], writes=[st6])

            def f2(eng, st6=st6, mv=mv):
                return eng.bn_aggr(out=mv.ap, in_=st6.ap)
            P.op("dve", f2, reads=[st6], writes=[mv])
            sdv = stat.v(tt_ % 4, slice(8, 9))
            act(sdv, stat.v(tt_ % 4, slice(7, 8)), AF.Sqrt, bias=eps_c)
            rsv = stat.v(tt_ % 4, slice(9, 10))
            recip(rsv, sdv)
            t2 = nxt("tmpB", tmpB)
            stt(t2.v(), vg.v(), stat.v(tt_ % 4, slice(6, 7)), sgun.v(e), ALU.subtract, ALU.mult)
            act(vln.v(tt_), t2.v(), AF.Identity, scale=rsv)
        for g in range(4):
            for H in range(2):
                b = bankA()

                def fsp(eng, b=b, g=g, H=H):
                    ins = None
                    for i in range(4):
                        ins = eng.matmul(psb[b][:, i * 128:(i + 1) * 128], lhsT=vln.v(4 * H + i, slice(g * 128, (g + 1) * 128)).ap,
                                         rhs=sguw.v(e, g).ap, start=True, stop=True)
                    return ins
                P.op("pe", fsp, reads=[vln.v(slice(4 * H, 4 * H + 4)), sguw.v(e, g)], writes=[PS(b)])
                t = nxt("tmpB", tmpB)
                _, lo, hi = sgub.rng((e, g))
                bc = sgub.raw(lo, [[0, 128], [0, 4], [1, 128]], lo, hi)
                tview = View(t.h[:, :].rearrange("p (a b) -> p a b", a=4), t.v().blk)
                pview = View(psb[b][:, :].rearrange("p (a b) -> p a b", a=4), PS(b).blk)
                tt(tview, pview, bc, ALU.add)
                tt(OT.v(g, slice(H * 512, (H + 1) * 512)), t.v(), gu.v(g, slice(H * 512, (H + 1) * 512)), ALU.mult)
        s = load_w([(lambda s: Wa[s].v(slice(None), slice(0, 256)), wview(W, 1536, 1792))])
        for th in range(2):
            sl = slice(th * 512, (th + 1) * 512)
            b = proj_fm(s, 0, th)
            normrope(b, th, qkg.v(slice(2 * e + 1, 2 * e + 2)), KT[0].v(sl, p=slice(0, 64)), KT[1].v(sl, p=slice(0, 64)),
                     kout=KoutE.v(sl))
        dma_out("sp", o_kTe[e], KoutE.v(), "koutE")
        for H in range(2):
            b = bankA()

            def fv(eng, b=b, H=H, s=s):
                ins = None
                for i in range(4):
                    tt_ = 4 * H + i
                    for kc in range(8):
                        ins = eng.matmul(psb[b][:, i * 128:(i + 1) * 128], lhsT=hT.v(kc, slice(tt_ * 128, (tt_ + 1) * 128)).ap,
                                         rhs=Wa[s].v(kc, slice(128, 256)).ap, start=(kc == 0), stop=(kc == 7))
                return ins
            P.op("pe", fv, reads=[hT.v(slice(None), slice(H * 512, (H + 1) * 512)), Wa[s].v()], writes=[PS(b)])
            pv = View(psb[b][:, :].rearrange("p (a b) -> p a b", a=4), PS(b).blk)
            act(VoutE.v(slice(4 * H, 4 * H + 4)), pv, AF.Copy)
            for kv in range(2):
                for par in range(2):
                    vcopy(VaugE.v(slice(4 * H, 4 * H + 4), kv * 2 + par, slice(par * 64, par * 64 + 64)),
                          VoutE.v(slice(4 * H, 4 * H + 4), slice(kv * 64, (kv + 1) * 64)))
        dma_out("sp", o_ve[e].rearrange("(a p) c -> p a c", p=128), VoutE.v(), "voutE")
        s = load_w([(lambda s: Wa[s].v(), wview(W, 1024, 1536))])
        for g in range(2):
            for c in range(2):
                for th in range(2):
                    sl = slice(th * 512, (th + 1) * 512)
                    b = proj_fm(s, (2 * g + c) * 128, th)
                    normrope(b, th, qkg.v(slice(2 * e, 2 * e + 1)), QT[2 * c].v(sl, p=slice(0, 64)), QT[2 * c + 1].v(sl, p=slice(0, 64)))
            steps = []
            for hl in range(4):
                h = 4 * g + hl
                for th in range(2):
                    ob = bankB()
                    for kt in range(10):
                        steps.append(dict(
                            S=(KT[g].v(slice(kt * 128, (kt + 1) * 128), p=slice(0, 64 + NAUG)),
                               QT[hl].v(slice(th * 512, (th + 1) * 512), p=slice(0, 64 + NAUG)), 512),
                            bias=None,
                            PV=(ob, slice(None), VaugE.v(kt, g * 2 + (h % 2)), kt == 0, kt == 9),
                            fin=((h % 2, 4 + h // 2, th) if kt == 9 else None)))
            run_attn_steps(steps)
        w_out_proj(d_ewout[e], yT_mix)

    ranges = na_tile_ranges()

    def odd_mixer(l):
        o = l // 2
        W = d_owin[o]
        memset(VaugO.v(), 1.0)
        for G in range(4):
            s_qk = load_w([(lambda s: Wa[s].v(slice(None), slice(0, 256)), wview(W, 256 * G, 256 * G + 256)),
                           (lambda s: Wa[s].v(slice(None), slice(256, 512)), wview(W, 1024 + 256 * G, 1024 + 256 * G + 256))])
            s_v = load_w([(lambda s: Wa[s].v(slice(None), slice(0, 256)), wview(W, 2048 + 256 * G, 2048 + 256 * G + 256))])
            ctx_ops = []
            for hl in range(4):
                ctx_ops.append(dma_in("pool", KT[hl].v(slice(T, T + 256), p=slice(0, 64)), d_ckTo[o, 4 * G + hl], "ctxk%d" % hl))
            for hl in range(4):
                par = hl % 2
                for i in range(2):
                    ctx_ops.append(dma_in("pool", VaugO.v(8 + i, hl, slice(par * 64, par * 64 + 64)),
                                          d_cvo[o, i * 128:(i + 1) * 128, (4 * G + hl) * 64:(4 * G + hl + 1) * 64], "ctxv%d" % (i * 4 + hl)))
            for hl in range(4):
                for a in range(2):
                    h = 4 * G + hl
                    src = bass.AP(d_rpbz, ((o * 16 + h) * 15 + (1 - a)) * 2048 + 15, [[31, 64], [2048, 14], [1, 64]])
                    dma_in("sp", BT.v(hl, p=slice(a * 64, (a + 1) * 64)), src, "bt%d" % (hl * 2 + a))
            _, lo, hi = cmask.rng(())
            cmb = cmask.raw(0, [[0, 128], [0, 56], [1, 64]], lo, hi)
            btv = View(BT.h[:, :, :, :].rearrange("p h t c -> p (h t) c"), BT.v().blk)
            if oddstop == 1:
                raise _Stop()
            if oddstop != 5:
                tt(btv, btv, cmb, ALU.add)
            if oddstop == 2:
                raise _Stop()
            P.ambient = tuple(ctx_ops)
            for c in range(2):
                for th in range(2):
                    sl = slice(th * 512, (th + 1) * 512)
                    b = proj_fm(s_qk, c * 128, th)
                    act(QT[2 * c].v(sl, p=slice(0, 64)), PS(b, slice(0, 64)), AF.Copy)
                    vcopy(QT[2 * c + 1].v(sl, p=slice(0, 64)), PS(b, slice(64, 128)))
            if oddstop == 21:
                raise _Stop()
            for c in range(2):
                for th in range(2):
                    sl = slice(th * 512, (th + 1) * 512)
                    b = proj_fm(s_qk, 256 + c * 128, th)
                    var = 0
                    act(KoutO.v(c, sl), PS(b), AF.Copy)
                    vcopy(KT[2 * c].v(sl, p=slice(0, 64)), KoutO.v(c, sl, p=slice(0, 64)))
                    vcopy(KT[2 * c + 1].v(sl, p=slice(0, 64)), KoutO.v(c, sl, p=slice(64, 128)))
                if not (var & 1):
                    dma_out("sp", o_kTo[o, (2 * G + c) * 128:(2 * G + c + 1) * 128, :], KoutO.v(c), "koutO%d" % c)
            if oddstop == 22:
                raise _Stop()
            for ch in range(2):
                for H in range(2):
                    b = bankA()

                    def fv(eng, b=b, H=H, ch=ch, s_v=s_v):
                        ins = None
                        for i in range(4):
                            tt_ = 4 * H + i
                            for kc in range(8):
                                ins = eng.matmul(psb[b][:, i * 128:(i + 1) * 128], lhsT=hT.v(kc, slice(tt_ * 128, (tt_ + 1) * 128)).ap,
                                                 rhs=Wa[s_v].v(kc, slice(ch * 128, ch * 128 + 128)).ap, start=(kc == 0), stop=(kc == 7))
                        return ins
                    P.op("pe", fv, reads=[hT.v(slice(None), slice(H * 512, (H + 1) * 512)), Wa[s_v].v()], writes=[PS(b)])
                    pv = View(psb[b][:, :].rearrange("p (a b) -> p a b", a=4), PS(b).blk)
                    act(VoutO.v(slice(4 * H, 4 * H + 4), slice(ch * 128, ch * 128 + 128)), pv, AF.Copy)
                    for k2 in range(2):
                        hl = 2 * ch + k2
                        vcopy(VaugO.v(slice(4 * H, 4 * H + 4), hl, slice((hl % 2) * 64, (hl % 2) * 64 + 64)),
                              VoutO.v(slice(4 * H, 4 * H + 4), slice(hl * 64, (hl + 1) * 64)))
            P.ambient = ()
            if oddstop == 23:
                raise _Stop()
            dma_out("sp", o_vo[o].rearrange("(a p) c -> p a c", p=128)[:, :, 256 * G:256 * G + 256], VoutO.v(), "voutO")
            if oddstop == 3:
                raise _Stop()
            steps = []
            for hl in range(4):
                h = 4 * G + hl
                ob = [bankB(), bankB()]
                hsteps = []
                for i in range(2):
                    for th in range(2):
                        hsteps.append(dict(
                            S=(KT[hl].v(slice(T + i * 128, T + (i + 1) * 128)), QT[hl].v(slice(th * 512, (th + 1) * 512)), 512),
                            bias=None, th=th,
                            PV=[ob[th], slice(None), VaugO.v(8 + i, hl), i == 0, False], fin=None))
                for i in range(8):
                    jlo, jhi = ranges[i]
                    q0, q1 = jlo * 128, (jhi + 1) * 128
                    for th in range(2):
                        c0, c1 = max(q0, th * 512), min(q1, (th + 1) * 512)
                        if c0 >= c1:
                            continue
                        w = c1 - c0
                        boff = (3 - i) * 128 + c0
                        _, lo, hi = BT.rng((hl,))
                        bias = BT.raw(lo + boff, [[0, 128], [1, w]], lo + boff, lo + boff + w - 1)
                        hsteps.append(dict(
                            S=(KT[hl].v(slice(i * 128, (i + 1) * 128)), QT[hl].v(slice(c0, c1)), w),
                            bias=bias, th=th,
                            PV=[ob[th], slice(c0 - th * 512, c1 - th * 512), VaugO.v(i, hl), False, False], fin=None))
                for th in range(2):
                    last = [st for st in hsteps if st["th"] == th][-1]
                    last["PV"][4] = True
                    last["fin"] = (hl % 2, h // 2, th)
                steps += hsteps
            run_attn_steps(steps)
            if oddstop == 4:
                raise _Stop()
        w_out_proj(d_owout[o], yT_mix)

    def ffn(l, modgen=None):
        W = d_fwin[l]
        for m in range(NJ // 2):
            if modgen is not None:
                next(modgen, None)
            s = load_w([(lambda s: Wa[s].v(slice(None), slice(0, 256)), wview(W, 256 * m, 256 * m + 256)),
                        (lambda s: Wa[s].v(slice(None), slice(256, 512)), wview(W, DFF + 256 * m, DFF + 256 * m + 256))])
            for th in range(2):
                for c in range(2):
                    j = 2 * m + c
                    sl = slice(th * 512, (th + 1) * 512)
                    bg = proj_fm(s, c * 128, th)
                    bu = proj_fm(s, 256 + c * 128, th)
                    sg = nxt("tmpB", tmpB)
                    act(sg.v(), PS(bg), AF.Silu)
                    tt(actT.v(j, sl), PS(bu), sg.v(), ALU.mult)
        if modgen is not None:
            for _ in modgen:
                pass
        Wo = d_fwout[l].rearrange("(j p) n -> p j n", p=128)
        for fc in range(8):
            s = load_w([(lambda s: Wb[s].v(), Wo[:, :, fc * 128:(fc + 1) * 128])])
            for th in range(2):
                b = bankA()
                mm_group(PS(b), [(Wb[s].v(j), actT.v(j, slice(th * 512, (th + 1) * 512))) for j in range(NJ)])
                evac(yT_ffn.v(fc, slice(th * 512, (th + 1) * 512)), PS(b))

    def _layers():
        for l in range(nlayers):
            mi = l % 2
            sub_in(mi, 0, 0)
            if l % 2 == 0:
                even_mixer(l)
            else:
                odd_mixer(l)
            sub_out(yT_mix, mi, 1)
            if stop == (l, "mix"):
                break
            sub_in(mi, 2, 24)
            modgen = modulation_gen(l + 1) if l + 1 < nlayers else None
            ffn(l, modgen)
            sub_out(yT_ffn, mi, 3)
            if stop == (l, "ffn"):
                break

    modulation(0)
    try:
        _layers()
    except _Stop:
        pass
    for fc in range(8):
        dma_out("sp", o_yT[:, fc, :], xT.v(fc), "yout")
    P.op("sp", None, extra_deps=list(out_dmas))
    P.finalize(nc, es)
    es.close()
    return nc, P, sbuf_used


def _rs(r):
    return min(max(r - 4, 0), 8)


def _const_tables(is_sample):
    t = np.arange(T)
    cosT = np.ones((128, T), np.float32)
    sinT = np.zeros((128, T), np.float32)
    if is_sample:
        freqs = (10000.0 ** (-np.arange(16, dtype=np.float32) / 16)).astype(np.float32)
        for p in range(128):
            i = p % 64
            f = i % 16
            pos = (t // 64) if i < 32 else (t % 64)
            ang = pos.astype(np.float32) * freqs[f]
            cosT[p] = np.cos(ang)
            sg = -1.0 if (i % 32) < 16 else 1.0
            sinT[p] = sg * np.sin(ang)
    kaug1 = np.zeros((NAUG, T + 256), np.float32)
    kaug1[t // 64, t] = 1.0
    kaug1[16, T:] = 1.0
    kaug = np.concatenate([kaug1, kaug1], 0)
    qE = np.zeros((NAUG, T), np.float32)
    qO = np.zeros((NAUG, T), np.float32)
    r = t // 64
    if is_sample:
        for j in range(16):
            rs = np.array([_rs(x) for x in r])
            valid = (rs <= j) & (j < rs + 8)
            qO[j] = np.where(valid, 0.0, -BIG)
    else:
        for j in range(16):
            qE[j] = np.where((j // 4) == (r // 4), 0.0, -BIG)
        qE[16] = -BIG
    qaug = np.concatenate([qE, qO], 0)
    cm = np.zeros((128, 64), np.float32)
    if is_sample:
        c = np.arange(64)
        cs = np.clip(c - 8, 0, 48)
        for p in range(128):
            cp = p % 64
            ok = (cp >= cs) & (cp < cs + 16)
            cm[p] = np.where(ok, 0.0, CMASK)
    return cosT, sinT, qaug.astype(np.float32), kaug.astype(np.float32), cm


def _perm():
    Pm = np.zeros((128, 128), np.float32)
    for p in range(128):
        q = p + 16 if (p % 32) < 16 else p - 16
        Pm[q, p] = 1.0
    return Pm


_CACHE = {}


def _get_nc():
    if "nc" not in _CACHE:
        nl = int(os.environ.get("MK_NLAYERS", "4"))
        stop = os.environ.get("MK_STOP")
        if stop:
            a, b = stop.split(",")
            stop = (int(a), b)
        _CACHE["nc"] = build_nc(nl, stop)[0]
    return _CACHE["nc"]


def prepare(x_prompt, x_sample, cache_attn_k, cache_attn_v, cache_na_k, cache_na_v, c, c_ctx,
           mod_w, mod_b, norm_mix_pre, norm_mix_post, norm_ffn_pre, norm_ffn_post,
           even_w_in, even_w_out, sgu_w, sgu_b, sgu_norm, q_norm, k_norm,
           odd_w_in, odd_w_out, na_rpb, ffn_w_in, ffn_w_out):
    f = lambda a: np.ascontiguousarray(np.asarray(a, dtype=np.float32))
    x_prompt, x_sample = f(x_prompt), f(x_sample)
    shared = {
        "mod_w": f(mod_w),
        "modbT": f(np.asarray(mod_b).reshape(4, 48, 128).transpose(2, 0, 1)),
        "gainsT": f(np.stack([np.asarray(g) for g in (norm_mix_pre, norm_mix_post, norm_ffn_pre, norm_ffn_post)], 0)
                    .reshape(4, 4, 8, 128).transpose(3, 0, 1, 2)),
        "even_w_in": f(even_w_in), "even_w_out": f(even_w_out),
        "odd_w_in": f(odd_w_in), "odd_w_out": f(odd_w_out),
        "ffn_w_in": f(ffn_w_in), "ffn_w_out": f(ffn_w_out),
        "sguwT": f(np.asarray(sgu_w).transpose(0, 1, 3, 2)),
        "sgub_bc": f(np.broadcast_to(np.asarray(sgu_b).reshape(1, -1), (128, 1024))),
        "sgun_bc": f(np.broadcast_to(np.asarray(sgu_norm).reshape(1, -1), (128, 1024))),
        "qkgain": f(np.stack([np.tile(np.asarray(q_norm)[0], 2), np.tile(np.asarray(k_norm)[0], 2),
                              np.tile(np.asarray(q_norm)[1], 2), np.tile(np.asarray(k_norm)[1], 2)], 1)),
        "perm": _perm(),
    }
    rp = np.asarray(na_rpb, dtype=np.float32)
    fr = np.flip(np.flip(rp, -1), -2)
    fr = np.pad(fr, ((0, 0), (0, 0), (0, 0), (0, 1)))
    rpbz_s = f(np.tile(fr, (1, 1, 1, 64)).reshape(-1))
    rpbz_p = np.zeros_like(rpbz_s)
    tabs = {False: _const_tables(False), True: _const_tables(True)}
    in_maps = []
    for core in range(8):
        smp = core >= 4
        if smp:
            b = core - 4
            xc = x_sample[b]
            cond = np.asarray(c)[b]
            ckTe = f(np.asarray(cache_attn_k)[b].transpose(0, 2, 3, 1).reshape(2, 128, 256))
            cve = f(np.asarray(cache_attn_v)[b].reshape(2, 256, 128))
            ckTo = f(np.asarray(cache_na_k)[b].transpose(0, 2, 3, 1))
            cvo = f(np.asarray(cache_na_v)[b].reshape(2, 256, 1024))
            rpbz = rpbz_s
        else:
            xc = x_prompt[4 * core:4 * core + 4].reshape(T, D)
            cond = np.asarray(c_ctx)
            ckTe = np.zeros((2, 128, 256), np.float32)
            cve = np.zeros((2, 256, 128), np.float32)
            ckTo = np.zeros((2, 16, 64, 256), np.float32)
            cvo = np.zeros((2, 256, 1024), np.float32)
            rpbz = rpbz_p
        cosT, sinT, qaug, kaug, cm = tabs[smp]
        m = dict(shared)
        m.update({
            "xT": f(xc.T.reshape(8, 128, T).transpose(1, 0, 2)),
            "condT": f(np.asarray(cond, dtype=np.float32).reshape(8, 128).T),
            "cosT": cosT, "sinT": sinT, "qaug": qaug, "kaug": kaug, "colmaskT": cm,
            "rpbz": rpbz, "ckT_e": ckTe, "cv_e": cve, "ckT_o": ckTo, "cv_o": cvo,
        })
        in_maps.append(m)
    return in_maps


def kernel(**inputs):
    in_maps = prepare(**inputs)
    nc = _get_nc()
    res = run_bass_kernel_spmd(nc, in_maps, core_ids=list(range(8)))
    return assemble(res.results)


def assemble(R):
    y_prompt = np.zeros((16, 256, D), np.float32)
    y_sample = np.zeros((4, T, D), np.float32)
    nak = np.zeros((16, 2, 256, 2, 64), np.float32)
    nav = np.zeros((16, 2, 256, 2, 64), np.float32)
    nnk = np.zeros((16, 2, 256, 16, 64), np.float32)
    nnv = np.zeros((16, 2, 256, 16, 64), np.float32)
    for core in range(8):
        r = R[core]
        y = np.asarray(r["yT"]).transpose(1, 0, 2).reshape(D, T).T
        if core >= 4:
            y_sample[core - 4] = y
        else:
            y_prompt[4 * core:4 * core + 4] = y.reshape(4, 256, D)
            kTe = np.asarray(r["kT_even"])
            nak[4 * core:4 * core + 4] = kTe.reshape(2, 2, 64, 4, 256).transpose(3, 0, 4, 1, 2)
            ve = np.asarray(r["v_even"])
            nav[4 * core:4 * core + 4] = ve.reshape(2, 4, 256, 2, 64).transpose(1, 0, 2, 3, 4)
            kTo = np.asarray(r["kT_odd"])
            nnk[4 * core:4 * core + 4] = kTo.reshape(2, 16, 64, 4, 256).transpose(3, 0, 4, 1, 2)
            vo = np.asarray(r["v_odd"])
            nnv[4 * core:4 * core + 4] = vo.reshape(2, 4, 256, 16, 64).transpose(1, 0, 2, 3, 4)
    return (y_prompt, y_sample, nak, nav, nnk, nnv)
```

```python
import contextlib
import math
import os
import numpy as np
import concourse.bass as bass
import concourse.mybir as mybir
from concourse.bass_utils import run_bass_kernel_spmd

F32 = mybir.dt.float32
BF16 = mybir.dt.bfloat16
AF = mybir.ActivationFunctionType
ALU = mybir.AluOpType

D = 1024
T = 1024
DFF = 2816
NJ = DFF // 128
EPS = 1e-6
BIG = 32768.0
CMASK = -30000.0
SB_BASE = 16896
SB_TOP = 229344
BS = 512
ENGS = ("pe", "act", "dve", "pool", "sp")
NAUG = 17
SMALL = 1 << 40


class View:
    __slots__ = ("ap", "blk", "n")

    def __init__(self, ap, blk):
        self.ap = ap
        self.blk = blk
        n = 1
        for d in tuple(ap.shape)[1:]:
            n *= int(d)
        self.n = n


class Buf:
    def __init__(self, nc, name, free_shape, dtype, off, parts=128):
        self.esz = 2 if dtype == BF16 else 4
        self.h = nc.alloc_sbuf_tensor_at(name, [parts] + list(free_shape), dtype, offset=off)
        self.off = off
        self.shape = list(free_shape)
        st = []
        s = 1
        for d in reversed(self.shape):
            st.insert(0, s)
            s *= d
        self.strides = st
        self.fsize = s
        self.nbytes = s * self.esz

    def rng(self, idx):
        idx = list(idx) + [slice(None)] * (len(self.shape) - len(idx))
        lo = hi = 0
        for i, d, st in zip(idx, self.shape, self.strides):
            if isinstance(i, slice):
                a = 0 if i.start is None else i.start
                b = d if i.stop is None else i.stop
                lo += a * st
                hi += (b - 1) * st
            else:
                lo += i * st
                hi += i * st
        return idx, lo, hi

    def blocks(self, lo, hi):
        b0 = (self.off + lo * self.esz) // BS
        b1 = (self.off + (hi + 1) * self.esz - 1) // BS
        return frozenset(("s", b) for b in range(b0, b1 + 1))

    def v(self, *idx, p=slice(None)):
        idx, lo, hi = self.rng(idx)
        return View(self.h[tuple([p] + idx)], self.blocks(lo, hi))

    def raw(self, elem_off, dims, lo, hi, p0=0):
        ap = bass.AP(self.h, p0 * self.fsize + elem_off, [[self.fsize, dims[0][1]]] + [list(x) for x in dims[1:]])
        return View(ap, self.blocks(lo, hi))


class Op:
    __slots__ = ("eng", "fn", "deps", "dma_key", "ticket", "signal", "idx", "size")


class Prog:
    def __init__(self):
        self.ops = []
        self.lastw = {}
        self.readers = {}
        self.group_keys = set()
        self.ambient = ()

    def op(self, eng, fn, reads=(), writes=(), dma_key=None, extra_deps=()):
        o = Op()
        o.eng = eng
        o.fn = fn
        o.dma_key = dma_key
        o.signal = False
        o.ticket = None
        o.idx = len(self.ops)
        o.size = min([v.n for v in writes] + [1 << 30])
        rb = set()
        wb = set()
        for v in reads:
            rb |= v.blk
        for v in writes:
            wb |= v.blk
        deps = set(extra_deps) | set(self.ambient)
        for b in rb:
            w = self.lastw.get(b)
            if w is not None:
                deps.add(w)
        for b in wb:
            w = self.lastw.get(b)
            if w is not None:
                deps.add(w)
            for r in self.readers.get(b, ()):
                deps.add(r)
        for b in wb:
            self.lastw[b] = o
            self.readers[b] = []
        for b in rb - wb:
            self.readers.setdefault(b, []).append(o)
        deps.discard(o)
        o.deps = deps
        self.ops.append(o)
        return o

    @staticmethod
    def _skip(d, o):
        if d.dma_key is not None or o.dma_key is not None or d.eng != o.eng:
            return False
        if d.eng == "pe":
            return True
        return d.size > SMALL and o.size > SMALL

    def finalize(self, nc, es):
        for o in self.ops:
            for d in o.deps:
                if not self._skip(d, o):
                    d.signal = True
        engcnt = {e: 0 for e in ENGS}
        keycnt = {}
        for o in self.ops:
            if o.dma_key is not None:
                keycnt[o.dma_key] = keycnt.get(o.dma_key, 0) + 16
                o.ticket = (("k", o.dma_key), keycnt[o.dma_key])
            elif o.signal and o.fn is not None:
                engcnt[o.eng] += 1
                o.ticket = (("e", o.eng), engcnt[o.eng])
        for o in self.ops:
            if o.dma_key is not None and o.dma_key in self.group_keys:
                o.ticket = (("k", o.dma_key), keycnt[o.dma_key])
        for o in self.ops:
            for d in o.deps:
                if d.dma_key is not None and d.dma_key in self.group_keys:
                    assert o.dma_key != d.dma_key, "group-key self dependency %s" % d.dma_key
        sems = {}
        for o in self.ops:
            if o.ticket is not None and o.ticket[0] not in sems:
                sems[o.ticket[0]] = es.enter_context(nc.semaphore("s_%s_%s" % o.ticket[0]))
        self.stats = (dict(engcnt), dict(keycnt), len(sems))
        block = es.enter_context(nc.Block())
        engmap = {"pe": block.tensor, "act": block.scalar, "dve": block.vector, "pool": block.gpsimd, "sp": block.sync}
        for en in ENGS:
            mine = [o for o in self.ops if o.eng == en]

            def f(eng, mine=mine):
                seen = {}
                for o in mine:
                    need = {}
                    for d in o.deps:
                        if self._skip(d, o) or d.ticket is None:
                            continue
                        sk, val = d.ticket
                        if need.get(sk, 0) < val:
                            need[sk] = val
                    for sk, val in need.items():
                        if seen.get(sk, 0) < val:
                            eng.wait_ge(sems[sk], val)
                            seen[sk] = val
                    if o.fn is None:
                        continue
                    ins = o.fn(eng)
                    if o.ticket is not None:
                        ins.then_inc(sems[o.ticket[0]], 16 if o.dma_key is not None else 1)

            engmap[en](f)


def na_tile_ranges():
    def rs(r):
        return min(max(r - 4, 0), 8)
    lo_hi = []
    for j in range(8):
        lo = rs(2 * j) // 2
        hi = (rs(2 * j + 1) + 7) // 2
        lo_hi.append((lo, hi))
    out = []
    for i in range(8):
        js = [j for j in range(8) if lo_hi[j][0] <= i <= lo_hi[j][1]]
        assert js == list(range(js[0], js[-1] + 1))
        for j in js:
            assert -3 <= i - j <= 3
        out.append((js[0], js[-1]))
    return out


class _Stop(Exception):
    pass


def build_nc(nlayers=4, stop=None):
    oddstop = int(os.environ.get("MK_ODDSTOP", "0"))
    nc = bass.Bass("TRN2", target_bir_lowering=False)
    P = Prog()
    es = contextlib.ExitStack()

    def din(name, shape):
        return nc.dram_tensor(name, list(shape), F32, kind="ExternalInput").ap()

    def dout(name, shape):
        return nc.dram_tensor(name, list(shape), F32, kind="ExternalOutput").ap()

    d_xT = din("xT", [128, 8, T])
    d_cond = din("condT", [128, 8])
    d_modw = din("mod_w", [4, D, 6 * D])
    d_modb = din("modbT", [128, 4, 48])
    d_gains = din("gainsT", [128, 4, 4, 8])
    d_ewin = din("even_w_in", [2, D, 1792])
    d_ewout = din("even_w_out", [2, D, D])
    d_owin = din("odd_w_in", [2, D, 3072])
    d_owout = din("odd_w_out", [2, D, D])
    d_fwin = din("ffn_w_in", [4, D, 2 * DFF])
    d_fwout = din("ffn_w_out", [4, DFF, D])
    d_sguw = din("sguwT", [2, 4, 128, 128])
    d_sgub = din("sgub_bc", [128, 2 * 4 * 128])
    d_sgun = din("sgun_bc", [128, 2 * 512])
    d_qkg = din("qkgain", [128, 4])
    d_perm = din("perm", [128, 128])
    d_cos = din("cosT", [128, T])
    d_sin = din("sinT", [128, T])
    d_qaug = din("qaug", [2 * NAUG, T])
    d_kaug = din("kaug", [2 * NAUG, T + 256])
    d_cmask = din("colmaskT", [128, 64])
    d_rpbz = nc.dram_tensor("rpbz", [2 * 16 * 15 * 2048], F32, kind="ExternalInput")
    d_ckTe = din("ckT_e", [2, 128, 256])
    d_cve = din("cv_e", [2, 256, 128])
    d_ckTo = din("ckT_o", [2, 16, 64, 256])
    d_cvo = din("cv_o", [2, 256, 1024])
    o_yT = dout("yT", [128, 8, T])
    o_kTe = dout("kT_even", [2, 128, T])
    o_ve = dout("v_even", [2, T, 128])
    o_kTo = dout("kT_odd", [2, D, T])
    o_vo = dout("v_odd", [2, T, D])

    cur = [SB_BASE]

    def alloc(nbytes):
        off = cur[0]
        cur[0] = off + ((nbytes + BS - 1) // BS) * BS
        assert cur[0] <= SB_TOP, "SBUF overflow %d" % (cur[0] - SB_TOP)
        return off

    def mk(name, shape, dt, parts=128, off=None):
        esz = 2 if dt == BF16 else 4
        n = esz
        for s in shape:
            n *= s
        if off is None:
            off = alloc(n)
        return Buf(nc, name, shape, dt, off, parts)

    xT = mk("xT", [8, T], F32)
    off_h = alloc(16384 * 2)
    hT = mk("hT", [8, T], BF16, off=off_h)
    OT = mk("OT", [8, T], BF16, off=off_h + 16384)
    yT_ffn = mk("yTf", [8, T], F32, off=off_h)
    QT = [mk("QT%d" % i, [T], BF16, parts=98) for i in range(4)]
    KT = [mk("KT%d" % i, [T + 256], BF16, parts=98) for i in range(4)]
    NSLOT = 3
    slot_off = [alloc(8192) for _ in range(NSLOT)]
    Wa = [mk("Wa%d" % i, [8, 512], BF16, off=slot_off[i]) for i in range(NSLOT)]
    Wb = [mk("Wb%d" % i, [NJ, 128], BF16, off=slot_off[i]) for i in range(NSLOT)]
    R0 = alloc(45056)
    actT = mk("actT", [NJ, T], BF16, off=R0)
    yT_mix = mk("yTm", [8, T], F32, off=R0)
    gu = mk("gu", [4, T], BF16, off=R0)
    vln = mk("vln", [8, 512], BF16, off=R0 + 8192)
    VaugE = mk("VaugE", [10, 4, 128], BF16, off=R0 + 16384)
    KoutE = mk("KoutE", [T], F32, off=R0 + 26624)
    VoutE = mk("VoutE", [8, 128], F32, off=R0 + 30720)
    VaugO = mk("VaugO", [10, 4, 128], BF16, off=R0)
    BT = mk("BT", [4, 14, 64], F32, off=R0 + 10240)
    KoutO = mk("KoutO", [2, T], F32, off=R0 + 24576)
    VoutO = mk("VoutO", [8, 256], F32, off=R0 + 32768)
    cosb = mk("cosb", [T], F32)
    sinb = mk("sinb", [T], F32)
    rstd = mk("rstd", [T], F32)
    sqb = [mk("sqb%d" % i, [512], BF16) for i in range(3)]
    tmpB = [mk("tmpB%d" % i, [512], F32) for i in range(6)]
    qgb = [mk("qgb%d" % i, [512], BF16) for i in range(2)]
    PTb = [mk("PT%d" % i, [512], BF16) for i in range(3)]
    sgub = mk("sgub", [2, 4, 128], F32)
    sgun = mk("sgun", [2, 512], F32)
    sguw = mk("sguw", [2, 4, 128], BF16)
    onesm = mk("onesm", [128], BF16)
    bdiag = mk("bdiag", [128], BF16)
    permb = mk("permb", [128], BF16)
    modT = mk("modT", [2, 48], F32)
    modb = mk("modb", [4, 48], F32)
    gains = mk("gains", [4, 4, 8], F32)
    der = mk("der", [2, 4, 8], F32)
    condb = mk("condb", [8], F32)
    scT = mk("scT", [8], BF16)
    misc = mk("misc", [8], F32)
    qkg = mk("qkg", [4], F32)
    cmask = mk("cmask", [64], F32)
    mrow = mk("mrow", [512], F32)
    stat = mk("stat", [4, 16], F32)
    sbuf_used = cur[0] - SB_BASE

    psb = [es.enter_context(nc.psum_tensor("psb%d" % i, [128, 512], F32)) for i in range(8)]

    def PS(i, p=slice(None), c=slice(None)):
        return View(psb[i][p, c], frozenset([("p", i)]))

    rotA = [0]
    rotB = [0]

    def bankA():
        b = rotA[0] % 4
        rotA[0] += 1
        return b

    def bankB():
        b = 4 + rotB[0] % 3
        rotB[0] += 1
        return b

    MBANK = 7

    rot = {}

    def nxt(name, lst):
        i = rot.get(name, 0)
        rot[name] = i + 1
        return lst[i % len(lst)]

    def mm_group(out, pairs, eng="pe", extra_reads=()):
        reads = [v for pr in pairs for v in pr] + list(extra_reads)

        def fn(e, out=out, pairs=pairs):
            n = len(pairs)
            ins = None
            for k, (l, r) in enumerate(pairs):
                ins = e.matmul(out.ap, lhsT=l.ap, rhs=r.ap, start=(k == 0), stop=(k == n - 1))
            return ins
        return P.op("pe", fn, reads=reads, writes=[out])

    def mm1(out, l, r, start=True, stop=True):
        def fn(e, out=out, l=l, r=r, start=start, stop=stop):
            return e.matmul(out.ap, lhsT=l.ap, rhs=r.ap, start=start, stop=stop)
        return P.op("pe", fn, reads=[l, r], writes=[out])

    def act(out, in_, func, scale=1.0, bias=None):
        reads = [in_]
        kw = {}
        if isinstance(scale, View):
            reads.append(scale)
            kw["scale"] = scale.ap
        else:
            kw["scale"] = float(scale)
        if isinstance(bias, View):
            reads.append(bias)
            kw["bias"] = bias.ap
        elif bias is not None:
            kw["bias"] = float(bias)

        def fn(e, out=out, in_=in_, func=func, kw=kw):
            return e.activation(out=out.ap, in_=in_.ap, func=func, **kw)
        return P.op("act", fn, reads=reads, writes=[out])

    def tt(out, a, b, op, eng="dve"):
        def fn(e, out=out, a=a, b=b, op=op):
            return e.tensor_tensor(out=out.ap, in0=a.ap, in1=b.ap, op=op)
        return P.op(eng, fn, reads=[a, b], writes=[out])

    def stt(out, a, s, b, op0, op1):
        reads = [a, b]
        if isinstance(s, View):
            reads.append(s)
            sv = s.ap
        else:
            sv = float(s)

        def fn(e, out=out, a=a, b=b, sv=sv, op0=op0, op1=op1):
            return e.scalar_tensor_tensor(out=out.ap, in0=a.ap, scalar=sv, in1=b.ap, op0=op0, op1=op1)
        return P.op("dve", fn, reads=reads, writes=[out])

    def ts(out, a, s1, s2, op0, op1):
        reads = [a]
        s1v = s1.ap if isinstance(s1, View) else float(s1)
        s2v = s2.ap if isinstance(s2, View) else float(s2)
        for s in (s1, s2):
            if isinstance(s, View):
                reads.append(s)

        def fn(e, out=out, a=a, s1v=s1v, s2v=s2v, op0=op0, op1=op1):
            return e.tensor_scalar(out=out.ap, in0=a.ap, scalar1=s1v, scalar2=s2v, op0=op0, op1=op1)
        return P.op("dve", fn, reads=reads, writes=[out])

    def vcopy(out, in_, eng="dve"):
        def fn(e, out=out, in_=in_):
            return e.tensor_copy(out=out.ap, in_=in_.ap)
        return P.op(eng, fn, reads=[in_], writes=[out])

    def recip(out, in_):
        def fn(e, out=out, in_=in_):
            return e.reciprocal(out=out.ap, in_=in_.ap)
        return P.op("dve", fn, reads=[in_], writes=[out])

    def memset(v, val, eng="dve"):
        def fn(e, v=v, val=val):
            return e.memset(v.ap, val)
        return P.op(eng, fn, writes=[v])

    def dma_in(q, out, src_ap, key):
        def fn(e, out=out, src_ap=src_ap):
            return e.dma_start(out=out.ap, in_=src_ap)
        return P.op(q, fn, writes=[out], dma_key=key)

    out_dmas = []

    def dma_out(q, dst_ap, in_, key):
        def fn(e, dst_ap=dst_ap, in_=in_):
            return e.dma_start(out=dst_ap, in_=in_.ap)
        o = P.op(q, fn, reads=[in_], dma_key=key)
        out_dmas.append(o)
        return o

    slot_i = [0]

    def load_w(parts):
        s = slot_i[0] % NSLOT
        slot_i[0] += 1
        for vf, src in parts:
            dma_in("pool", vf(s), src, "w%d" % s)
        return s

    def wview(dram2d, c0, c1):
        return dram2d.rearrange("(kc p) n -> p kc n", p=128)[:, :, c0:c1]

    P.group_keys.add("const")
    for fc in range(8):
        dma_in("sp", xT.v(fc), d_xT[:, fc, :], "const")
    dma_in("sp", condb.v(), d_cond, "const")
    dma_in("sp", modb.v(), d_modb, "const")
    dma_in("sp", gains.v(), d_gains, "const")
    dma_in("sp", sgub.v(), d_sgub.rearrange("p (e g q) -> p e g q", e=2, g=4), "const")
    dma_in("sp", sgun.v(), d_sgun.rearrange("p (e c) -> p e c", e=2), "const")
    dma_in("sp", qkg.v(), d_qkg, "const")
    dma_in("sp", cosb.v(), d_cos, "const")
    dma_in("sp", sinb.v(), d_sin, "const")
    dma_in("sp", cmask.v(), d_cmask, "const")
    P.group_keys.add("constc")
    dma_in("pool", permb.v(), d_perm, "constc")
    dma_in("pool", sguw.v(), d_sguw.rearrange("e g q p -> q e g p"), "constc")
    for i in range(4):
        dma_in("pool", QT[i].v(p=slice(64, 98)), d_qaug, "constc")
        dma_in("pool", KT[i].v(p=slice(64, 98)), d_kaug, "constc")
    memset(onesm.v(), 1.0 / 1024.0)
    memset(bdiag.v(), 0.0)
    memset(bdiag.v(slice(0, 64), p=slice(0, 64)), 1.0 / 64.0)
    memset(bdiag.v(slice(64, 128), p=slice(64, 128)), 1.0 / 64.0)
    memset(misc.v(slice(0, 1)), EPS)
    memset(misc.v(slice(1, 2)), 1.0)
    eps_c = misc.v(slice(0, 1))
    one_c = misc.v(slice(1, 2), p=slice(0, 1))
    act(scT.v(), condb.v(), AF.Silu)

    def modulation_gen(l):
        mi = l % 2
        mbank = MBANK
        for n in range(12):
            s = load_w([(lambda s: Wa[s].v(), wview(d_modw[l], n * 512, (n + 1) * 512))])
            b = bankA()
            mm_group(PS(b, slice(0, 1)), [(scT.v(slice(kc, kc + 1)), Wa[s].v(kc)) for kc in range(8)])
            vcopy(mrow.v(p=slice(0, 1)), PS(b, slice(0, 1)))
            for i in range(4):
                col = n * 4 + i
                mm1(PS(mbank, slice(None), slice(col, col + 1)), mrow.v(slice(i * 128, (i + 1) * 128), p=slice(0, 1)), one_c)
            yield
        tt(modT.v(mi), PS(mbank, slice(None), slice(0, 48)), modb.v(l), ALU.add)
        m = lambda a, b: modT.v(mi, slice(a, b))
        stt(der.v(mi, 0), m(8, 16), 1.0, gains.v(0, l), ALU.add, ALU.mult)
        tt(der.v(mi, 1), m(16, 24), gains.v(1, l), ALU.mult)
        stt(der.v(mi, 2), m(32, 40), 1.0, gains.v(2, l), ALU.add, ALU.mult)
        tt(der.v(mi, 3), m(40, 48), gains.v(3, l), ALU.mult)

    def modulation(l):
        for _ in modulation_gen(l):
            pass

    def col(buf, *idx):
        return buf.v(*idx)

    def rms_stats(src):
        banks = [bankB(), bankB()]
        for th in range(2):
            for fc in range(8):
                sq = nxt("sqb", sqb)
                act(sq.v(), src.v(fc, slice(th * 512, (th + 1) * 512)), AF.Square)
                mm1(PS(banks[th]), onesm.v(), sq.v(), start=(fc == 0), stop=(fc == 7))
            t = nxt("tmpB", tmpB)
            act(t.v(), PS(banks[th]), AF.Sqrt, bias=eps_c)
            recip(rstd.v(slice(th * 512, (th + 1) * 512)), t.v())

    def sub_in(mi, ai, bcol0):
        rms_stats(xT)
        for th in range(2):
            for fc in range(8):
                sl = slice(th * 512, (th + 1) * 512)
                t = nxt("tmpB", tmpB)
                stt(t.v(), xT.v(fc, sl), der.v(mi, ai, slice(fc, fc + 1)), rstd.v(sl), ALU.mult, ALU.mult)
                act(hT.v(fc, sl), t.v(), AF.Identity, bias=modT.v(mi, slice(bcol0 + fc, bcol0 + fc + 1)))

    def sub_out(yb, mi, gi):
        rms_stats(yb)
        for th in range(2):
            for fc in range(8):
                sl = slice(th * 512, (th + 1) * 512)
                t = nxt("tmpB", tmpB)
                stt(t.v(), yb.v(fc, sl), der.v(mi, gi, slice(fc, fc + 1)), rstd.v(sl), ALU.mult, ALU.mult)
                tt(xT.v(fc, sl), xT.v(fc, sl), t.v(), ALU.add)

    def proj_fm(ws, c0, th, src=None, nk=8, wb=None):
        src = hT if src is None else src
        b = bankA()
        wbuf = Wa[ws] if wb is None else wb
        mm_group(PS(b), [(wbuf.v(kc, slice(c0, c0 + 128)), src.v(kc, slice(th * 512, (th + 1) * 512))) for kc in range(nk)])
        return b

    evac_flip = [0]

    def evac(out, bank_view):
        evac_flip[0] ^= 1
        if evac_flip[0]:
            act(out, bank_view, AF.Copy)
        else:
            vcopy(out, bank_view)

    def w_out_proj(dram2d, ybuf):
        for half in range(2):
            s = load_w([(lambda s: Wa[s].v(), wview(dram2d, half * 512, (half + 1) * 512))])
            for th in range(2):
                for c in range(4):
                    fc = half * 4 + c
                    b = proj_fm(s, c * 128, th, src=OT)
                    evac(ybuf.v(fc, slice(th * 512, (th + 1) * 512)), PS(b))

    def normrope(b, th, gcol, dstA, dstB, kout=None):
        sl = slice(th * 512, (th + 1) * 512)
        sq = nxt("sqb", sqb)
        act(sq.v(), PS(b), AF.Square)
        qg = nxt("qgb", qgb)
        a_last = act(qg.v(), PS(b), AF.Identity, scale=gcol)
        P.ambient = (a_last,)
        bs = bankB()
        mm1(PS(bs), bdiag.v(), sq.v())
        bp = bankB()
        mm1(PS(bp), permb.v(), qg.v())
        sd = nxt("tmpB", tmpB)
        act(sd.v(), PS(bs), AF.Sqrt, bias=eps_c)
        rs = nxt("tmpB", tmpB)
        recip(rs.v(), sd.v())
        if kout is not None:
            stt(kout, PS(b), gcol, rs.v(), ALU.mult, ALU.mult)
        t1 = nxt("tmpB", tmpB)
        stt(t1.v(), PS(b), gcol, cosb.v(sl), ALU.mult, ALU.mult)
        t2 = nxt("tmpB", tmpB)
        tt(t2.v(), PS(bp), sinb.v(sl), ALU.mult)
        tt(t1.v(), t1.v(), t2.v(), ALU.add)
        tt(dstA, t1.v(p=slice(0, 64)), rs.v(p=slice(0, 64)), ALU.mult)
        tt(dstB, t1.v(p=slice(64, 128)), rs.v(p=slice(64, 128)), ALU.mult)
        P.ambient = ()

    def finalize_head(obank, parity, chunk, th):
        sl = slice(th * 512, (th + 1) * 512)
        r = nxt("tmpB", tmpB)
        if parity == 0:
            recip(r.v(p=slice(0, 64)), PS(obank, slice(64, 128)))
            tt(OT.v(chunk, sl, p=slice(0, 64)), PS(obank, slice(0, 64)), r.v(p=slice(0, 64)), ALU.mult)
        else:
            recip(r.v(p=slice(64, 128)), PS(obank, slice(0, 64)))
            tt(OT.v(chunk, sl, p=slice(64, 128)), PS(obank, slice(64, 128)), r.v(p=slice(64, 128)), ALU.mult)

    attn_mod = [None, 12]

    def run_attn_steps(steps):
        n = len(steps)
        sb = [None] * n
        pt = [None] * n
        for k in range(n + 2):
            if k < n:
                st = steps[k]
                b = bankA()
                sb[k] = b
                l, r, w = st["S"]
                mm1(PS(b, slice(None), slice(0, w)), l, r)
            if 0 <= k - 1 < n:
                st = steps[k - 1]
                w = st["S"][2]
                p = nxt("PT", PTb)
                pt[k - 1] = p
                if st["bias"] is None:
                    act(p.v(slice(0, w)), PS(sb[k - 1], slice(None), slice(0, w)), AF.Exp, scale=0.125)
                else:
                    t = nxt("tmpB", tmpB)
                    stt(t.v(slice(0, w)), PS(sb[k - 1], slice(None), slice(0, w)), 0.125, st["bias"], ALU.mult, ALU.add)
                    act(p.v(slice(0, w)), t.v(slice(0, w)), AF.Exp)
            if 0 <= k - 2 < n:
                st = steps[k - 2]
                w = st["S"][2]
                ob, oc, vl, sta, sto = st["PV"]
                mm1(PS(ob, slice(None), oc), vl, pt[k - 2].v(slice(0, w)), start=sta, stop=sto)
                if st.get("fin") is not None:
                    finalize_head(ob, *st["fin"])
            if attn_mod[0] is not None and k % attn_mod[1] == attn_mod[1] - 1:
                next(attn_mod[0], None)

    def even_mixer(l):
        e = l // 2
        W = d_ewin[e]
        memset(VaugE.v(), 1.0)
        for kv in range(2):
            dma_in("pool", KT[kv].v(slice(T, T + 256), p=slice(0, 64)), d_ckTe[e, kv * 64:(kv + 1) * 64, :], "ctxk%d" % kv)
        for i in range(2):
            for kv in range(2):
                for par in range(2):
                    dma_in("pool", VaugE.v(8 + i, kv * 2 + par, slice(par * 64, par * 64 + 64)),
                           d_cve[e, i * 128:(i + 1) * 128, kv * 64:(kv + 1) * 64], "ctxv%d" % (i * 4 + kv * 2 + par))
        s = load_w([(lambda s: Wa[s].v(), wview(W, 0, 512))])
        for th in range(2):
            for c in range(4):
                b = proj_fm(s, c * 128, th)
                act(gu.v(c, slice(th * 512, (th + 1) * 512)), PS(b), AF.Gelu_apprx_tanh)
        s = load_w([(lambda s: Wa[s].v(), wview(W, 512, 1024))])
        for tt_ in range(8):
            b = bankA()
            mm_group(PS(b), [(hT.v(kc, slice(tt_ * 128, (tt_ + 1) * 128)), Wa[s].v(kc)) for kc in range(8)])
            vg = nxt("tmpB", tmpB)
            act(vg.v(), PS(b), AF.Gelu_apprx_tanh)
            sv = stat.v(tt_ % 4)
            st6 = stat.v(tt_ % 4, slice(0, 6))
            mv = stat.v(tt_ % 4, slice(6, 8))

            def f1(eng, st6=st6, vg=vg):
                return eng.bn_stats(out=st6.ap, in_=vg.v().ap)
            P.op("dve", f1, reads=[vg.v()], writes=[st6])

            def f2(eng, st6=st6, mv=mv):
                return eng.bn_aggr(out=mv.ap, in_=st6.ap)
            P.op("dve", f2, reads=[st6], writes=[mv])
            sdv = stat.v(tt_ % 4, slice(8, 9))
            act(sdv, stat.v(tt_ % 4, slice(7, 8)), AF.Sqrt, bias=eps_c)
            rsv = stat.v(tt_ % 4, slice(9, 10))
            recip(rsv, sdv)
            t2 = nxt("tmpB", tmpB)
            stt(t2.v(), vg.v(), stat.v(tt_ % 4, slice(6, 7)), sgun.v(e), ALU.subtract, ALU.mult)
            act(vln.v(tt_), t2.v(), AF.Identity, scale=rsv)
        for g in range(4):
            for H in range(2):
                b = bankA()

                def fsp(eng, b=b, g=g, H=H):
                    ins = None
                    for i in range(4):
                        ins = eng.matmul(psb[b][:, i * 128:(i + 1) * 128], lhsT=vln.v(4 * H + i, slice(g * 128, (g + 1) * 128)).ap,
                                         rhs=sguw.v(e, g).ap, start=True, stop=True)
                    return ins
                P.op("pe", fsp, reads=[vln.v(slice(4 * H, 4 * H + 4)), sguw.v(e, g)], writes=[PS(b)])
                t = nxt("tmpB", tmpB)
                _, lo, hi = sgub.rng((e, g))
                bc = sgub.raw(lo, [[0, 128], [0, 4], [1, 128]], lo, hi)
                tview = View(t.h[:, :].rearrange("p (a b) -> p a b", a=4), t.v().blk)
                pview = View(psb[b][:, :].rearrange("p (a b) -> p a b", a=4), PS(b).blk)
                tt(tview, pview, bc, ALU.add)
                tt(OT.v(g, slice(H * 512, (H + 1) * 512)), t.v(), gu.v(g, slice(H * 512, (H + 1) * 512)), ALU.mult)
        s = load_w([(lambda s: Wa[s].v(slice(None), slice(0, 256)), wview(W, 1536, 1792))])
        for th in range(2):
            sl = slice(th * 512, (th + 1) * 512)
            b = proj_fm(s, 0, th)
            normrope(b, th, qkg.v(slice(2 * e + 1, 2 * e + 2)), KT[0].v(sl, p=slice(0, 64)), KT[1].v(sl, p=slice(0, 64)),
                     kout=KoutE.v(sl))
        dma_out("sp", o_kTe[e], KoutE.v(), "koutE")
        for H in range(2):
            b = bankA()

            def fv(eng, b=b, H=H, s=s):
                ins = None
                for i in range(4):
                    tt_ = 4 * H + i
                    for kc in range(8):
                        ins = eng.matmul(psb[b][:, i * 128:(i + 1) * 128], lhsT=hT.v(kc, slice(tt_ * 128, (tt_ + 1) * 128)).ap,
                                         rhs=Wa[s].v(kc, slice(128, 256)).ap, start=(kc == 0), stop=(kc == 7))
                return ins
            P.op("pe", fv, reads=[hT.v(slice(None), slice(H * 512, (H + 1) * 512)), Wa[s].v()], writes=[PS(b)])
            pv = View(psb[b][:, :].rearrange("p (a b) -> p a b", a=4), PS(b).blk)
            act(VoutE.v(slice(4 * H, 4 * H + 4)), pv, AF.Copy)
            for kv in range(2):
                for par in range(2):
                    vcopy(VaugE.v(slice(4 * H, 4 * H + 4), kv * 2 + par, slice(par * 64, par * 64 + 64)),
                          VoutE.v(slice(4 * H, 4 * H + 4), slice(kv * 64, (kv + 1) * 64)))
        dma_out("sp", o_ve[e].rearrange("(a p) c -> p a c", p=128), VoutE.v(), "voutE")
        for g in range(2):
            s = load_w([(lambda s: Wa[s].v(slice(None), slice(0, 256)), wview(W, 1024 + 256 * g, 1024 + 256 * g + 256))])
            for c in range(2):
                for th in range(2):
                    sl = slice(th * 512, (th + 1) * 512)
                    b = proj_fm(s, c * 128, th)
                    normrope(b, th, qkg.v(slice(2 * e, 2 * e + 1)), QT[2 * c].v(sl, p=slice(0, 64)), QT[2 * c + 1].v(sl, p=slice(0, 64)))
            steps = []
            for hl in range(4):
                h = 4 * g + hl
                for th in range(2):
                    ob = bankB()
                    for kt in range(10):
                        steps.append(dict(
                            S=(KT[g].v(slice(kt * 128, (kt + 1) * 128), p=slice(0, 64 + NAUG)),
                               QT[hl].v(slice(th * 512, (th + 1) * 512), p=slice(0, 64 + NAUG)), 512),
                            bias=None,
                            PV=(ob, slice(None), VaugE.v(kt, g * 2 + (h % 2)), kt == 0, kt == 9),
                            fin=((h % 2, 4 + h // 2, th) if kt == 9 else None)))
            run_attn_steps(steps)
        w_out_proj(d_ewout[e], yT_mix)

    ranges = na_tile_ranges()

    def odd_mixer(l):
        o = l // 2
        W = d_owin[o]
        memset(VaugO.v(), 1.0)
        for G in range(4):
            s_qk = load_w([(lambda s: Wa[s].v(slice(None), slice(0, 256)), wview(W, 256 * G, 256 * G + 256)),
                           (lambda s: Wa[s].v(slice(None), slice(256, 512)), wview(W, 1024 + 256 * G, 1024 + 256 * G + 256))])
            s_v = load_w([(lambda s: Wa[s].v(slice(None), slice(0, 256)), wview(W, 2048 + 256 * G, 2048 + 256 * G + 256))])
            ctx_ops = []
            for hl in range(4):
                ctx_ops.append(dma_in("pool", KT[hl].v(slice(T, T + 256), p=slice(0, 64)), d_ckTo[o, 4 * G + hl], "ctxk%d" % hl))
            for hl in range(4):
                par = hl % 2
                for i in range(2):
                    ctx_ops.append(dma_in("pool", VaugO.v(8 + i, hl, slice(par * 64, par * 64 + 64)),
                                          d_cvo[o, i * 128:(i + 1) * 128, (4 * G + hl) * 64:(4 * G + hl + 1) * 64], "ctxv%d" % (i * 4 + hl)))
            for hl in range(4):
                for a in range(2):
                    h = 4 * G + hl
                    src = bass.AP(d_rpbz, ((o * 16 + h) * 15 + (1 - a)) * 2048 + 15, [[31, 64], [2048, 14], [1, 64]])
                    dma_in("sp", BT.v(hl, p=slice(a * 64, (a + 1) * 64)), src, "bt%d" % (hl * 2 + a))
            _, lo, hi = cmask.rng(())
            cmb = cmask.raw(0, [[0, 128], [0, 56], [1, 64]], lo, hi)
            btv = View(BT.h[:, :, :, :].rearrange("p h t c -> p (h t) c"), BT.v().blk)
            if oddstop == 1:
                raise _Stop()
            if oddstop != 5:
                tt(btv, btv, cmb, ALU.add)
            if oddstop == 2:
                raise _Stop()
            P.ambient = tuple(ctx_ops)
            for c in range(2):
                for th in range(2):
                    sl = slice(th * 512, (th + 1) * 512)
                    b = proj_fm(s_qk, c * 128, th)
                    act(QT[2 * c].v(sl, p=slice(0, 64)), PS(b, slice(0, 64)), AF.Copy)
                    vcopy(QT[2 * c + 1].v(sl, p=slice(0, 64)), PS(b, slice(64, 128)))
            if oddstop == 21:
                raise _Stop()
            for c in range(2):
                for th in range(2):
                    sl = slice(th * 512, (th + 1) * 512)
                    b = proj_fm(s_qk, 256 + c * 128, th)
                    var = 0
                    act(KoutO.v(c, sl), PS(b), AF.Copy)
                    vcopy(KT[2 * c].v(sl, p=slice(0, 64)), KoutO.v(c, sl, p=slice(0, 64)))
                    vcopy(KT[2 * c + 1].v(sl, p=slice(0, 64)), KoutO.v(c, sl, p=slice(64, 128)))
                if not (var & 1):
                    dma_out("sp", o_kTo[o, (2 * G + c) * 128:(2 * G + c + 1) * 128, :], KoutO.v(c), "koutO%d" % c)
            if oddstop == 22:
                raise _Stop()
            for ch in range(2):
                for H in range(2):
                    b = bankA()

                    def fv(eng, b=b, H=H, ch=ch, s_v=s_v):
                        ins = None
                        for i in range(4):
                            tt_ = 4 * H + i
                            for kc in range(8):
                                ins = eng.matmul(psb[b][:, i * 128:(i + 1) * 128], lhsT=hT.v(kc, slice(tt_ * 128, (tt_ + 1) * 128)).ap,
                                                 rhs=Wa[s_v].v(kc, slice(ch * 128, ch * 128 + 128)).ap, start=(kc == 0), stop=(kc == 7))
                        return ins
                    P.op("pe", fv, reads=[hT.v(slice(None), slice(H * 512, (H + 1) * 512)), Wa[s_v].v()], writes=[PS(b)])
                    pv = View(psb[b][:, :].rearrange("p (a b) -> p a b", a=4), PS(b).blk)
                    act(VoutO.v(slice(4 * H, 4 * H + 4), slice(ch * 128, ch * 128 + 128)), pv, AF.Copy)
                    for k2 in range(2):
                        hl = 2 * ch + k2
                        vcopy(VaugO.v(slice(4 * H, 4 * H + 4), hl, slice((hl % 2) * 64, (hl % 2) * 64 + 64)),
                              VoutO.v(slice(4 * H, 4 * H + 4), slice(hl * 64, (hl + 1) * 64)))
            P.ambient = ()
            if oddstop == 23:
                raise _Stop()
            dma_out("sp", o_vo[o].rearrange("(a p) c -> p a c", p=128)[:, :, 256 * G:256 * G + 256], VoutO.v(), "voutO")
            if oddstop == 3:
                raise _Stop()
            steps = []
            for hl in range(4):
                h = 4 * G + hl
                ob = [bankB(), bankB()]
                hsteps = []
                for i in range(2):
                    for th in range(2):
                        hsteps.append(dict(
                            S=(KT[hl].v(slice(T + i * 128, T + (i + 1) * 128)), QT[hl].v(slice(th * 512, (th + 1) * 512)), 512),
                            bias=None, th=th,
                            PV=[ob[th], slice(None), VaugO.v(8 + i, hl), i == 0, False], fin=None))
                for i in range(8):
                    jlo, jhi = ranges[i]
                    q0, q1 = jlo * 128, (jhi + 1) * 128
                    for th in range(2):
                        c0, c1 = max(q0, th * 512), min(q1, (th + 1) * 512)
                        if c0 >= c1:
                            continue
                        w = c1 - c0
                        boff = (3 - i) * 128 + c0
                        _, lo, hi = BT.rng((hl,))
                        bias = BT.raw(lo + boff, [[0, 128], [1, w]], lo + boff, lo + boff + w - 1)
                        hsteps.append(dict(
                            S=(KT[hl].v(slice(i * 128, (i + 1) * 128)), QT[hl].v(slice(c0, c1)), w),
                            bias=bias, th=th,
                            PV=[ob[th], slice(c0 - th * 512, c1 - th * 512), VaugO.v(i, hl), False, False], fin=None))
                for th in range(2):
                    last = [st for st in hsteps if st["th"] == th][-1]
                    last["PV"][4] = True
                    last["fin"] = (hl % 2, h // 2, th)
                steps += hsteps
            run_attn_steps(steps)
            if oddstop == 4:
                raise _Stop()
        w_out_proj(d_owout[o], yT_mix)

    def ffn(l, modgen=None):
        W = d_fwin[l]
        for m in range(NJ // 2):
            if modgen is not None:
                next(modgen, None)
            s = load_w([(lambda s: Wa[s].v(slice(None), slice(0, 256)), wview(W, 256 * m, 256 * m + 256)),
                        (lambda s: Wa[s].v(slice(None), slice(256, 512)), wview(W, DFF + 256 * m, DFF + 256 * m + 256))])
            for th in range(2):
                for c in range(2):
                    j = 2 * m + c
                    sl = slice(th * 512, (th + 1) * 512)
                    bg = proj_fm(s, c * 128, th)
                    bu = proj_fm(s, 256 + c * 128, th)
                    sg = nxt("tmpB", tmpB)
                    act(sg.v(), PS(bg), AF.Silu)
                    tt(actT.v(j, sl), PS(bu), sg.v(), ALU.mult)
        if modgen is not None:
            for _ in modgen:
                pass
        Wo = d_fwout[l].rearrange("(j p) n -> p j n", p=128)
        for fc in range(8):
            s = load_w([(lambda s: Wb[s].v(), Wo[:, :, fc * 128:(fc + 1) * 128])])
            for th in range(2):
                b = bankA()
                mm_group(PS(b), [(Wb[s].v(j), actT.v(j, slice(th * 512, (th + 1) * 512))) for j in range(NJ)])
                evac(yT_ffn.v(fc, slice(th * 512, (th + 1) * 512)), PS(b))

    def _layers():
        for l in range(nlayers):
            mi = l % 2
            sub_in(mi, 0, 0)
            attn_mod[0] = modulation_gen(l + 1) if l + 1 < nlayers else None
            attn_mod[1] = 12 if l % 2 == 0 else 20
            if l % 2 == 0:
                even_mixer(l)
            else:
                odd_mixer(l)
            if attn_mod[0] is not None:
                for _ in attn_mod[0]:
                    pass
                attn_mod[0] = None
            sub_out(yT_mix, mi, 1)
            if stop == (l, "mix"):
                break
            sub_in(mi, 2, 24)
            ffn(l, None)
            sub_out(yT_ffn, mi, 3)
            if stop == (l, "ffn"):
                break

    modulation(0)
    try:
        _layers()
    except _Stop:
        pass
    for fc in range(8):
        dma_out("sp", o_yT[:, fc, :], xT.v(fc), "yout")
    P.op("sp", None, extra_deps=list(out_dmas))
    P.finalize(nc, es)
    es.close()
    return nc, P, sbuf_used


def _rs(r):
    return min(max(r - 4, 0), 8)


def _const_tables(is_sample):
    t = np.arange(T)
    cosT = np.ones((128, T), np.float32)
    sinT = np.zeros((128, T), np.float32)
    if is_sample:
        freqs = (10000.0 ** (-np.arange(16, dtype=np.float32) / 16)).astype(np.float32)
        for p in range(128):
            i = p % 64
            f = i % 16
            pos = (t // 64) if i < 32 else (t % 64)
            ang = pos.astype(np.float32) * freqs[f]
            cosT[p] = np.cos(ang)
            sg = -1.0 if (i % 32) < 16 else 1.0
            sinT[p] = sg * np.sin(ang)
    kaug1 = np.zeros((NAUG, T + 256), np.float32)
    kaug1[t // 64, t] = 1.0
    kaug1[16, T:] = 1.0
    kaug = np.concatenate([kaug1, kaug1], 0)
    qE = np.zeros((NAUG, T), np.float32)
    qO = np.zeros((NAUG, T), np.float32)
    r = t // 64
    if is_sample:
        for j in range(16):
            rs = np.array([_rs(x) for x in r])
            valid = (rs <= j) & (j < rs + 8)
            qO[j] = np.where(valid, 0.0, -BIG)
    else:
        for j in range(16):
            qE[j] = np.where((j // 4) == (r // 4), 0.0, -BIG)
        qE[16] = -BIG
    qaug = np.concatenate([qE, qO], 0)
    cm = np.zeros((128, 64), np.float32)
    if is_sample:
        c = np.arange(64)
        cs = np.clip(c - 8, 0, 48)
        for p in range(128):
            cp = p % 64
            ok = (cp >= cs) & (cp < cs + 16)
            cm[p] = np.where(ok, 0.0, CMASK)
    return cosT, sinT, qaug.astype(np.float32), kaug.astype(np.float32), cm


def _perm():
    Pm = np.zeros((128, 128), np.float32)
    for p in range(128):
        q = p + 16 if (p % 32) < 16 else p - 16
        Pm[q, p] = 1.0
    return Pm


_CACHE = {}


def _get_nc():
    if "nc" not in _CACHE:
        nl = int(os.environ.get("MK_NLAYERS", "4"))
        stop = os.environ.get("MK_STOP")
        if stop:
            a, b = stop.split(",")
            stop = (int(a), b)
        _CACHE["nc"] = build_nc(nl, stop)[0]
    return _CACHE["nc"]


def prepare(x_prompt, x_sample, cache_attn_k, cache_attn_v, cache_na_k, cache_na_v, c, c_ctx,
           mod_w, mod_b, norm_mix_pre, norm_mix_post, norm_ffn_pre, norm_ffn_post,
           even_w_in, even_w_out, sgu_w, sgu_b, sgu_norm, q_norm, k_norm,
           odd_w_in, odd_w_out, na_rpb, ffn_w_in, ffn_w_out):
    f = lambda a: np.ascontiguousarray(np.asarray(a, dtype=np.float32))
    x_prompt, x_sample = f(x_prompt), f(x_sample)
    shared = {
        "mod_w": f(mod_w),
        "modbT": f(np.asarray(mod_b).reshape(4, 48, 128).transpose(2, 0, 1)),
        "gainsT": f(np.stack([np.asarray(g) for g in (norm_mix_pre, norm_mix_post, norm_ffn_pre, norm_ffn_post)], 0)
                    .reshape(4, 4, 8, 128).transpose(3, 0, 1, 2)),
        "even_w_in": f(even_w_in), "even_w_out": f(even_w_out),
        "odd_w_in": f(odd_w_in), "odd_w_out": f(odd_w_out),
        "ffn_w_in": f(ffn_w_in), "ffn_w_out": f(ffn_w_out),
        "sguwT": f(np.asarray(sgu_w).transpose(0, 1, 3, 2)),
        "sgub_bc": f(np.broadcast_to(np.asarray(sgu_b).reshape(1, -1), (128, 1024))),
        "sgun_bc": f(np.broadcast_to(np.asarray(sgu_norm).reshape(1, -1), (128, 1024))),
        "qkgain": f(np.stack([np.tile(np.asarray(q_norm)[0], 2), np.tile(np.asarray(k_norm)[0], 2),
                              np.tile(np.asarray(q_norm)[1], 2), np.tile(np.asarray(k_norm)[1], 2)], 1)),
        "perm": _perm(),
    }
    rp = np.asarray(na_rpb, dtype=np.float32)
    fr = np.flip(np.flip(rp, -1), -2)
    fr = np.pad(fr, ((0, 0), (0, 0), (0, 0), (0, 1)))
    rpbz_s = f(np.tile(fr, (1, 1, 1, 64)).reshape(-1))
    rpbz_p = np.zeros_like(rpbz_s)
    tabs = {False: _const_tables(False), True: _const_tables(True)}
    in_maps = []
    for core in range(8):
        smp = core >= 4
        if smp:
            b = core - 4
            xc = x_sample[b]
            cond = np.asarray(c)[b]
            ckTe = f(np.asarray(cache_attn_k)[b].transpose(0, 2, 3, 1).reshape(2, 128, 256))
            cve = f(np.asarray(cache_attn_v)[b].reshape(2, 256, 128))
            ckTo = f(np.asarray(cache_na_k)[b].transpose(0, 2, 3, 1))
            cvo = f(np.asarray(cache_na_v)[b].reshape(2, 256, 1024))
            rpbz = rpbz_s
        else:
            xc = x_prompt[4 * core:4 * core + 4].reshape(T, D)
            cond = np.asarray(c_ctx)
            ckTe = np.zeros((2, 128, 256), np.float32)
            cve = np.zeros((2, 256, 128), np.float32)
            ckTo = np.zeros((2, 16, 64, 256), np.float32)
            cvo = np.zeros((2, 256, 1024), np.float32)
            rpbz = rpbz_p
        cosT, sinT, qaug, kaug, cm = tabs[smp]
        m = dict(shared)
        m.update({
            "xT": f(xc.T.reshape(8, 128, T).transpose(1, 0, 2)),
            "condT": f(np.asarray(cond, dtype=np.float32).reshape(8, 128).T),
            "cosT": cosT, "sinT": sinT, "qaug": qaug, "kaug": kaug, "colmaskT": cm,
            "rpbz": rpbz, "ckT_e": ckTe, "cv_e": cve, "ckT_o": ckTo, "cv_o": cvo,
        })
        in_maps.append(m)
    return in_maps


def kernel(**inputs):
    in_maps = prepare(**inputs)
    nc = _get_nc()
    res = run_bass_kernel_spmd(nc, in_maps, core_ids=list(range(8)))
    return assemble(res.results)


def assemble(R):
    y_prompt = np.zeros((16, 256, D), np.float32)
    y_sample = np.zeros((4, T, D), np.float32)
    nak = np.zeros((16, 2, 256, 2, 64), np.float32)
    nav = np.zeros((16, 2, 256, 2, 64), np.float32)
    nnk = np.zeros((16, 2, 256, 16, 64), np.float32)
    nnv = np.zeros((16, 2, 256, 16, 64), np.float32)
    for core in range(8):
        r = R[core]
        y = np.asarray(r["yT"]).transpose(1, 0, 2).reshape(D, T).T
        if core >= 4:
            y_sample[core - 4] = y
        else:
            y_prompt[4 * core:4 * core + 4] = y.reshape(4, 256, D)
            kTe = np.asarray(r["kT_even"])
            nak[4 * core:4 * core + 4] = kTe.reshape(2, 2, 64, 4, 256).transpose(3, 0, 4, 1, 2)
            ve = np.asarray(r["v_even"])
            nav[4 * core:4 * core + 4] = ve.reshape(2, 4, 256, 2, 64).transpose(1, 0, 2, 3, 4)
            kTo = np.asarray(r["kT_odd"])
            nnk[4 * core:4 * core + 4] = kTo.reshape(2, 16, 64, 4, 256).transpose(3, 0, 4, 1, 2)
            vo = np.asarray(r["v_odd"])
            nnv[4 * core:4 * core + 4] = vo.reshape(2, 4, 256, 16, 64).transpose(1, 0, 2, 3, 4)
    return (y_prompt, y_sample, nak, nav, nnk, nnv)
```
